# Optimizing a Trainium2 kernel written in Bass

```python
import math
import jax, jax.numpy as jnp
from jax import lax
import numpy as np

D_MODEL = 4096
BATCH = 4
SEQ = 4096
DEPTH = 4

WA = D_MODEL // 2
HDA = 128
HA = WA // (2 * HDA)
WB = D_MODEL // 2
DKB = 128
HB = WB // DKB
DVB = WB // HB
DFF = 7 * D_MODEL // 4
CONV_W = 3
ROPE_THETA = 10000.0
Q_BLOCK = 128
CHUNK = 64
EPS = 1e-6
N_IN = 3 * WA + 5 * WB + 2 * D_MODEL

kernel_name = "hybrid_diffattn_hgrn2_convffn_encoder"


def rmsnorm(x, w):
    xf = x.astype(jnp.float32)
    y = xf * lax.rsqrt(jnp.mean(xf * xf, axis=-1, keepdims=True) + EPS)
    return (y * w.astype(jnp.float32)).astype(x.dtype)


def rope_tables(seq, dim):
    inv_freq = 1.0 / (ROPE_THETA ** (jnp.arange(0, dim, 2, dtype=jnp.float32) / dim))
    ang = jnp.arange(seq, dtype=jnp.float32)[:, None] * inv_freq[None, :]
    return jnp.cos(ang), jnp.sin(ang)


def apply_rope(x, cos, sin):
    xf = x.astype(jnp.float32)
    half = xf.shape[-1] // 2
    x1, x2 = xf[..., :half], xf[..., half:]
    c = cos[None, :, None, :]
    s = sin[None, :, None, :]
    return jnp.concatenate([x1 * c - x2 * s, x2 * c + x1 * s], axis=-1).astype(x.dtype)


def diff_attention(q, k, v, lam, subln_w, lambda_init):
    b, s, _, _ = q.shape
    nb = s // Q_BLOCK
    scale = HDA ** -0.5
    q_blocks = q.reshape(b, nb, Q_BLOCK, 2 * HA, HDA).transpose(1, 0, 3, 2, 4)
    kh = k.transpose(0, 2, 1, 3)
    vh = v.transpose(0, 2, 1, 3)

    def one_block(qb):
        sc = jnp.einsum('bhqd,bhkd->bhqk', qb, kh, preferred_element_type=jnp.float32) * scale
        p = jax.nn.softmax(sc, axis=-1).reshape(b, HA, 2, Q_BLOCK, s)
        att = p[:, :, 0] - lam * p[:, :, 1]
        return jnp.einsum('bhqk,bhkd->bhqd', att, vh.astype(jnp.float32))

    o = lax.map(one_block, q_blocks)
    o = o.transpose(1, 0, 3, 2, 4).reshape(b, s, HA, 2 * HDA)
    o = rmsnorm(o, subln_w) * (1.0 - lambda_init)
    return o.reshape(b, s, HA * 2 * HDA)


def _to_chunks(t, n_chunks):
    b, s, h, d = t.shape
    return t.reshape(b, n_chunks, CHUNK, h, d).transpose(1, 0, 3, 2, 4)


def hgrn2_scan(q, k, v, logf):
    b, s, h, dk = q.shape
    dv = v.shape[-1]
    nc = s // CHUNK
    mask = jnp.tril(jnp.ones((CHUNK, CHUNK), dtype=bool))[None, None, :, :, None]

    def step(state, inp):
        qc, kc, vc, gc = inp
        cum = jnp.cumsum(gc, axis=2)
        o_inter = jnp.einsum('bhtk,bhkv->bhtv', qc * jnp.exp(cum), state)
        rel = jnp.where(mask, cum[:, :, :, None, :] - cum[:, :, None, :, :], -jnp.inf)
        scores = jnp.einsum('bhtk,bhsk,bhtsk->bhts', qc, kc, jnp.exp(rel))
        o_intra = jnp.einsum('bhts,bhsv->bhtv', scores, vc)
        last = cum[:, :, -1:, :]
        new_state = (jnp.exp(last[:, :, 0, :])[..., None] * state
                     + jnp.einsum('bhsk,bhsv->bhkv', kc * jnp.exp(last - cum), vc))
        return new_state, o_inter + o_intra

    state0 = jnp.zeros((b, h, dk, dv), jnp.float32)
    xs = tuple(_to_chunks(t.astype(jnp.float32), nc) for t in (q, k, v, logf))
    _, o = lax.scan(step, state0, xs)
    return o.transpose(1, 0, 3, 2, 4).reshape(b, s, h, dv)


def forget_gate(f_pre, lb):
    f = f_pre.astype(jnp.float32)
    log_f = jnp.logaddexp(jnp.log(lb), jnp.log1p(-lb) + jax.nn.log_sigmoid(f))
    k = (1.0 - lb) * jax.nn.sigmoid(-f)
    return k, log_f


def hgrn2_mixer(q_pre, f_fwd, f_bwd, i_in, g_pre, lb, norm_w):
    b, s, _ = q_pre.shape
    heads = lambda t: t.reshape(b, s, HB, -1)
    q = heads(jax.nn.silu(q_pre.astype(jnp.float32)))
    v = heads(i_in.astype(jnp.float32))
    k_f, lf_f = forget_gate(f_fwd, lb[0])
    k_b, lf_b = forget_gate(f_bwd, lb[1])
    o_fwd = hgrn2_scan(q, heads(k_f), v, heads(lf_f))
    flip = lambda t: jnp.flip(t, axis=1)
    o_bwd = flip(hgrn2_scan(flip(q), flip(heads(k_b)), flip(v), flip(heads(lf_b))))
    o = rmsnorm(o_fwd + o_bwd, norm_w) * jax.nn.silu(heads(g_pre.astype(jnp.float32)))
    return o.reshape(b, s, WB)


def conv_ffn(h, w_up, conv_w, conv_b, w_down):
    s = h.shape[1]
    u = h @ w_up
    up = jnp.pad(u, ((0, 0), (1, 1), (0, 0)))
    u = up[:, 0:s] * conv_w[0] + up[:, 1:s + 1] * conv_w[1] + up[:, 2:s + 2] * conv_w[2] + conv_b
    gate, val = jnp.split(u, 2, axis=-1)
    return (jax.nn.silu(gate) * val) @ w_down


def setup_inputs(seed: int = 0) -> dict:
    key = jax.random.key(seed)
    ks = jax.random.split(key, 16)
    f32 = jnp.float32
    nrm = lambda k, shape, scale: jax.random.normal(k, shape, f32) * scale
    return {
        "x": nrm(ks[0], (BATCH, SEQ, D_MODEL), 1.0),
        "norm1_w": 1.0 + nrm(ks[1], (DEPTH, D_MODEL), 0.02),
        "w_in": nrm(ks[2], (DEPTH, D_MODEL, N_IN), D_MODEL ** -0.5),
        "diff_lambda": nrm(ks[3], (DEPTH, 4, HDA), 0.1),
        "diff_subln_w": 1.0 + nrm(ks[4], (DEPTH, 2 * HDA), 0.02),
        "hgrn_lb_logits": nrm(ks[5], (DEPTH, 2, WB), 0.1),
        "hgrn_norm_w": 1.0 + nrm(ks[6], (DEPTH, DVB), 0.02),
        "w_branch_a": nrm(ks[7], (DEPTH, WA, D_MODEL), WA ** -0.5),
        "w_branch_b": nrm(ks[8], (DEPTH, WB, D_MODEL), WB ** -0.5),
        "w_out": nrm(ks[9], (DEPTH, D_MODEL, D_MODEL), D_MODEL ** -0.5),
        "norm2_w": 1.0 + nrm(ks[10], (DEPTH, D_MODEL), 0.02),
        "w_up": nrm(ks[11], (DEPTH, D_MODEL, 2 * DFF), D_MODEL ** -0.5),
        "conv_w": nrm(ks[12], (DEPTH, CONV_W, 2 * DFF), CONV_W ** -0.5),
        "conv_b": nrm(ks[13], (DEPTH, 2 * DFF), 0.01),
        "w_down": nrm(ks[14], (DEPTH, DFF, D_MODEL), DFF ** -0.5),
        "final_norm_w": 1.0 + nrm(ks[15], (D_MODEL,), 0.02),
    }


def reference(x, norm1_w, w_in, diff_lambda, diff_subln_w, hgrn_lb_logits, hgrn_norm_w,
              w_branch_a, w_branch_b, w_out, norm2_w, w_up, conv_w, conv_b, w_down, final_norm_w):
    b, s, _ = x.shape
    cos, sin = rope_tables(s, HDA)
    lb_sm = jax.nn.softmax(hgrn_lb_logits.astype(jnp.float32), axis=0)
    lb_all = jnp.cumsum(lb_sm, axis=0) - lb_sm[0:1]
    sizes = [WA, WA, WA, WB, WB, WB, WB, WB, D_MODEL, D_MODEL]
    split_idx = list(np.cumsum(sizes)[:-1])
    for l in range(DEPTH):
        h = rmsnorm(x, norm1_w[l])
        proj = h @ w_in[l]
        qa, ka, va, qb, fb_f, fb_b, ib, gb, gate_a, gate_b = jnp.split(proj, split_idx, axis=-1)
        lambda_init = 0.8 - 0.6 * math.exp(-0.3 * l)
        lam_p = diff_lambda[l].astype(jnp.float32)
        lam = (jnp.exp(jnp.sum(lam_p[0] * lam_p[1])) - jnp.exp(jnp.sum(lam_p[2] * lam_p[3]))
               + lambda_init)
        qa = apply_rope(qa.reshape(b, s, 2 * HA, HDA), cos, sin)
        ka = apply_rope(ka.reshape(b, s, 2 * HA, HDA), cos, sin)
        va = va.reshape(b, s, HA, 2 * HDA)
        a_out = diff_attention(qa, ka, va, lam, diff_subln_w[l], lambda_init).astype(x.dtype)
        b_out = hgrn2_mixer(qb, fb_f, fb_b, ib, gb, lb_all[l], hgrn_norm_w[l]).astype(x.dtype)
        merged = (jax.nn.sigmoid(gate_a) * (a_out @ w_branch_a[l])
                  + jax.nn.sigmoid(gate_b) * (b_out @ w_branch_b[l]))
        x = x + merged @ w_out[l]
        h = rmsnorm(x, norm2_w[l])
        x = x + conv_ffn(h, w_up[l], conv_w[l], conv_b[l], w_down[l]).astype(x.dtype)
    return rmsnorm(x, final_norm_w)
```

```python
from contextlib import ExitStack
import numpy as np
import concourse.bass as bass
import concourse.mybir as mybir
from concourse.bass_utils import run_bass_kernel_spmd

F32 = mybir.dt.float32
BF16 = mybir.dt.bfloat16
AF = mybir.ActivationFunctionType
ALU = mybir.AluOpType


class Buf:
    __slots__ = ("name", "w", "r")

    def __init__(self, name=""):
        self.name = name
        self.w = {}
        self.r = {}


class Q:
    def __init__(self, cx, eng, name, n_dma_sems=0):
        self.cx = cx
        self.e = eng
        self.name = name
        self.sem = cx.new_sem("q_" + name)
        self.count = 0
        self.seen = {}
        self.dsems = [[cx.new_sem(f"d_{name}{i}"), 0] for i in range(n_dma_sems)]
        self.dnext = 0


class Ctx:
    def __init__(self, nc):
        self.nc = nc
        self.stack = ExitStack()
        self.nsem = 0
        self.pe = Q(self, nc.tensor, "pe")
        self.act = Q(self, nc.scalar, "act", 12)
        self.dve = Q(self, nc.vector, "dve")
        self.pool = Q(self, nc.gpsimd, "pool", 24)
        self.sp = Q(self, nc.sync, "sp", 40)
        self.uid = 0

    def new_sem(self, name):
        self.nsem += 1
        return self.stack.enter_context(self.nc.semaphore(name))

    def sbuf(self, shape, dtype, name=None):
        self.uid += 1
        return self.stack.enter_context(
            self.nc.sbuf_tensor(name or f"sb{self.uid}", list(shape), dtype))

    def psum(self, shape, dtype, name=None):
        self.uid += 1
        return self.stack.enter_context(
            self.nc.psum_tensor(name or f"ps{self.uid}", list(shape), dtype))

    def _waits(self, q, reads, writes, extra=None):
        need = {}
        for b in reads:
            for s, v in b.w.items():
                if need.get(s, 0) < v:
                    need[s] = v
        for b in writes:
            for d in (b.w, b.r):
                for s, v in d.items():
                    if need.get(s, 0) < v:
                        need[s] = v
        if extra:
            for s, v in extra:
                if need.get(s, 0) < v:
                    need[s] = v
        for s, v in need.items():
            if s is q.sem and q is self.pe:
                continue
            if q.seen.get(s, 0) >= v:
                continue
            q.e.wait_ge(s, v)
            q.seen[s] = v

    def op(self, q, fn, reads=(), writes=(), inc=True):
        self._waits(q, reads, writes)
        ins = fn(q.e)
        val = q.count + 1
        if inc:
            ins.then_inc(q.sem, 1)
            q.count = val
        for b in writes:
            b.w = {q.sem: val}
            b.r = {}
        for b in reads:
            if b.r.get(q.sem, 0) < val:
                b.r[q.sem] = val
        return ins

    def dma(self, q, out, in_, reads=(), writes=(), **kw):
        slot = q.dsems[q.dnext]
        q.dnext = (q.dnext + 1) % len(q.dsems)
        sem, cnt = slot
        self._waits(q, reads, writes, extra=[(sem, cnt)] if cnt else None)
        ins = q.e.dma_start(out=out, in_=in_, **kw)
        cnt += 16
        slot[1] = cnt
        ins.then_inc(sem, 16)
        for b in writes:
            b.w = {sem: cnt}
            b.r = {}
        for b in reads:
            b.r[sem] = cnt
        return ins

    def barrier(self):
        need = {}
        qs = (self.pe, self.act, self.dve, self.pool, self.sp)
        for qq in qs:
            if qq.count:
                need[qq.sem] = qq.count
            for s, c in qq.dsems:
                if c:
                    need[s] = c
        for q in qs:
            for s, v in need.items():
                if s is q.sem:
                    continue
                if q.seen.get(s, 0) < v:
                    q.e.wait_ge(s, v)
                    q.seen[s] = v

    def finish(self, bufs):
        q = self.sp
        need = {}
        for qq in (self.pe, self.act, self.dve, self.pool, self.sp):
            if qq.count:
                need[qq.sem] = qq.count
            for s, c in qq.dsems:
                if c:
                    need[s] = c
        for s, v in need.items():
            if q.seen.get(s, 0) < v:
                q.e.wait_ge(s, v)
                q.seen[s] = v


class Rot:
    def __init__(self, items):
        self.items = items
        self.i = 0

    def next(self):
        it = self.items[self.i]
        self.i = (self.i + 1) % len(self.items)
        return it


class Cfg:
    def __init__(self, D=4096, S=4096, DEPTH=4):
        self.D = D
        self.S = S
        self.DEPTH = DEPTH
        self.WA = D // 2
        self.HDA = 128
        self.HA = self.WA // 256
        self.WB = D // 2
        self.HB = self.WB // 128
        self.DFF = 7 * D // 4
        self.N_IN = 3 * self.WA + 5 * self.WB + 2 * D
        self.EPS = 1e-6
        self.CHUNK = 64


def rmsnorm_phase(cx, cfg, x_dram, xbuf, w_sb, h_dram, hbuf, consts, out_f32=False):
    nc = cx.nc
    D, S = cfg.D, cfg.S
    DC = D // 128
    TBK = 512 if DC <= 32 else 256
    xv = x_dram.rearrange("(c p) s -> p c s", p=128)
    hv = h_dram.rearrange("(c p) s -> p c s", p=128)
    with ExitStack() as st:
        old = cx.stack
        cx.stack = st
        xt = [(cx.sbuf([128, DC, TBK], F32), Buf()) for _ in range(1)]
        ht = [(cx.sbuf([128, DC, TBK], F32 if out_f32 else BF16), Buf()) for _ in range(1)]
        sq = Rot([(cx.sbuf([128, TBK], F32), Buf()) for _ in range(3)])
        rstd = (cx.sbuf([128, TBK], F32), Buf())
        ps = (cx.psum([128, TBK], F32), Buf())
        for tb in range(S // TBK):
            xs, xb = xt[0]
            hs, hb = ht[0]
            sl = slice(tb * TBK, (tb + 1) * TBK)
            cx.dma(cx.sp, xs[:], xv[:, :, sl], reads=[xbuf], writes=[xb])
            for c in range(DC):
                sqs, sqb = sq.next()
                cx.op(cx.act, lambda e: e.activation(out=sqs[:], in_=xs[:, c, :], func=AF.Square),
                      reads=[xb], writes=[sqb])
                cx.op(cx.pe, lambda e: e.matmul(ps[0][:], lhsT=consts["ones_f32"][:], rhs=sqs[:],
                                                start=(c == 0), stop=(c == DC - 1)),
                      reads=[sqb], writes=[ps[1]])
            cx.op(cx.act, lambda e: e.activation(out=rstd[0][:], in_=ps[0][:], func=AF.Sqrt,
                                                 bias=consts["eps"][:], scale=1.0 / D),
                  reads=[ps[1]], writes=[rstd[1]])
            cx.op(cx.dve, lambda e: e.reciprocal(out=rstd[0][:], in_=rstd[0][:]),
                  reads=[rstd[1]], writes=[rstd[1]])
            for c in range(DC):
                cx.op(cx.dve, lambda e: e.scalar_tensor_tensor(
                    out=hs[:, c, :], in0=xs[:, c, :], scalar=w_sb[:, c:c + 1], in1=rstd[0][:],
                    op0=ALU.mult, op1=ALU.mult),
                    reads=[xb, rstd[1]], writes=[hb])
            cx.dma(cx.sp, hv[:, :, sl], hs[:], reads=[hb], writes=[hbuf])
        cx.barrier()
        cx.stack = old


def gemm_phase(cx, cfg, W, xin, xin_buf, K, N, S, epilogue, TB=None, CW=512):
    KC = K // 128
    if TB is None:
        TB = 1024 if KC <= 32 else 512
    TB = min(TB, S)
    CW = min(CW, N)
    while N % CW:
        CW //= 2
    SBK = min(512, TB)
    Wv = W.rearrange("(kc p) n -> p kc n", p=128)
    Xv = xin.rearrange("(kc p) s -> p kc s", p=128)
    with ExitStack() as st:
        old = cx.stack
        cx.stack = st
        xblk = (cx.sbuf([128, KC, TB], BF16), Buf())
        wsl = Rot([(cx.sbuf([128, KC, CW], BF16), Buf()) for _ in range(2)])
        banks = Rot([(cx.psum([128, 512], F32), Buf()) for _ in range(4)])
        NG = N // CW
        for tb in range(S // TB):
            cx.dma(cx.sp, xblk[0][:], Xv[:, :, tb * TB:(tb + 1) * TB], reads=[xin_buf], writes=[xblk[1]])
            cur = wsl.next()
            cx.dma(cx.pool, cur[0][:], Wv[:, :, 0:CW], writes=[cur[1]])
            for ng in range(NG):
                nxt = None
                if ng + 1 < NG:
                    nxt = wsl.next()
                    cx.dma(cx.pool, nxt[0][:], Wv[:, :, (ng + 1) * CW:(ng + 2) * CW], writes=[nxt[1]])
                ws, wb = cur
                for nt in range(CW // 128):
                    for sb in range(TB // SBK):
                        pt, pb = banks.next()
                        for kc in range(KC):
                            cx.op(cx.pe, lambda e: e.matmul(
                                pt[:, :SBK], lhsT=ws[:, kc, nt * 128:(nt + 1) * 128],
                                rhs=xblk[0][:, kc, sb * SBK:(sb + 1) * SBK],
                                start=(kc == 0), stop=(kc == KC - 1)),
                                reads=[wb, xblk[1]], writes=[pb], inc=(kc == KC - 1))
                        epilogue(pt, pb, ng * CW + nt * 128, tb * TB + sb * SBK, SBK)
                cur = nxt
        cx.barrier()
        cx.stack = old


class Scope:
    def __init__(self, cx):
        self.cx = cx

    def __enter__(self):
        self.st = ExitStack()
        self.st.__enter__()
        self.old = self.cx.stack
        self.cx.stack = self.st
        return self

    def __exit__(self, *a):
        self.cx.barrier()
        self.cx.stack = self.old
        return self.st.__exit__(*a)


def make_evac(cx):
    flip = [0]

    def evac(out, in_, reads, writes, func=None, scale=1.0):
        flip[0] ^= 1
        if func is not None or flip[0]:
            f = func if func is not None else AF.Copy
            cx.op(cx.act, lambda e: e.activation(out=out, in_=in_, func=f, scale=scale),
                  reads=reads, writes=writes)
        else:
            cx.op(cx.dve, lambda e: e.tensor_copy(out=out, in_=in_), reads=reads, writes=writes)
    return evac


def proj_phase(cx, cfg, l, T):
    D, S, WA, WB = cfg.D, cfg.S, cfg.WA, cfg.WB
    segs = []
    o = 0
    for name, w in (("qa", WA), ("ka", WA), ("va", WA), ("qb", WB), ("ff", WB), ("fb", WB),
                    ("ib", WB), ("gb", WB), ("ga", D), ("gg", D)):
        segs.append((o, o + w, name))
        o += w
    pT = T["pT"].ap()
    sgT = T["sgT"].ap()
    cosv, sinv = T["c_cos"].ap(), T["c_sin"].ap()
    with Scope(cx):
        st_bf = Rot([(cx.sbuf([128, 512], BF16), Buf()) for _ in range(4)])
        st_f = Rot([(cx.sbuf([128, 512], F32), Buf()) for _ in range(3)])
        xr = Rot([(cx.sbuf([128, 512], BF16), Buf()) for _ in range(2)])
        t1 = Rot([(cx.sbuf([128, 512], F32), Buf()) for _ in range(2)])
        t2 = Rot([(cx.sbuf([128, 512], F32), Buf()) for _ in range(2)])
        CSW = min(1024, S)
        cs = [(cx.sbuf([128, CSW], F32), Buf()), (cx.sbuf([128, CSW], F32), Buf())]
        rps = (cx.psum([128, 512], F32), Buf())
        cs_t0 = [-1]

        def epi(pt, pb, n0, t0, w):
            kind = [k for (a, b, k) in segs if a <= n0 < b][0]
            import os as _os
            dbg = _os.environ.get("PROJ_DBG", "")
            if dbg == "simple":
                kind = "va"
            elif dbg == "norope" and kind in ("qa", "ka"):
                kind = "va"
            elif dbg == "nosg" and kind in ("ff", "fb"):
                kind = "va"
            if kind in ("qa", "ka"):
                cblk = t0 // CSW
                co = t0 % CSW
                if cs_t0[0] != cblk:
                    cs_t0[0] = cblk
                    cx.dma(cx.sp, cs[0][0][:], cosv[:, cblk * CSW:(cblk + 1) * CSW], writes=[cs[0][1]])
                    cx.dma(cx.sp, cs[1][0][:], sinv[:, cblk * CSW:(cblk + 1) * CSW], writes=[cs[1][1]])
                xs, xb = xr.next()
                cx.op(cx.act, lambda e: e.activation(out=xs[:, :w], in_=pt[:, :w], func=AF.Copy),
                      reads=[pb], writes=[xb])
                cx.op(cx.pe, lambda e: e.matmul(rps[0][:, :w], lhsT=T["rot"][:], rhs=xs[:, :w],
                                                start=True, stop=True), reads=[xb], writes=[rps[1]])
                a, ab = t1.next()
                cx.op(cx.dve, lambda e: e.tensor_tensor(out=a[:, :w], in0=xs[:, :w], in1=cs[0][0][:, co:co + w],
                                                        op=ALU.mult), reads=[xb, cs[0][1]], writes=[ab])
                b_, bb = t2.next()
                cx.op(cx.dve, lambda e: e.tensor_tensor(out=b_[:, :w], in0=rps[0][:, :w], in1=cs[1][0][:, co:co + w],
                                                        op=ALU.mult), reads=[rps[1], cs[1][1]], writes=[bb])
                ss, sb_ = st_bf.next()
                cx.op(cx.dve, lambda e: e.tensor_tensor(out=ss[:, :w], in0=a[:, :w], in1=b_[:, :w], op=ALU.add),
                      reads=[ab, bb], writes=[sb_])
                cx.dma(cx.sp, pT[n0:n0 + 128, t0:t0 + w], ss[:, :w], reads=[sb_])
            elif kind in ("ff", "fb"):
                ss, sb_ = st_f.next()
                cx.op(cx.act, lambda e: e.activation(out=ss[:, :w], in_=pt[:, :w], func=AF.Sigmoid),
                      reads=[pb], writes=[sb_])
                r0 = n0 - segs[4][0]
                cx.dma(cx.sp, sgT[r0:r0 + 128, t0:t0 + w], ss[:, :w], reads=[sb_])
            else:
                f = {"va": AF.Copy, "ib": AF.Copy, "qb": AF.Silu, "gb": AF.Silu,
                     "ga": AF.Sigmoid, "gg": AF.Sigmoid}[kind]
                ss, sb_ = st_bf.next()
                if f == AF.Copy and (n0 // 128) % 2:
                    cx.op(cx.dve, lambda e: e.tensor_copy(out=ss[:, :w], in_=pt[:, :w]), reads=[pb], writes=[sb_])
                else:
                    cx.op(cx.act, lambda e: e.activation(out=ss[:, :w], in_=pt[:, :w], func=f),
                          reads=[pb], writes=[sb_])
                cx.dma(cx.sp, pT[n0:n0 + 128, t0:t0 + w], ss[:, :w], reads=[sb_])

        gemm_phase(cx, cfg, T["w_in"].ap()[l], T["hT"].ap(), Buf(), D, cfg.N_IN, S, epi)


def attn_phase(cx, cfg, l, T, P):
    S, WA, HA = cfg.S, cfg.WA, cfg.HA
    KB = S // 128
    QBW = min(512, S)
    NQB = S // QBW
    pT = T["pT"].ap()
    aT = T["aT"].ap()
    scale = 128 ** -0.5
    lam_init = 0.8 - 0.6 * float(np.exp(-0.3 * l))
    with Scope(cx):
        lp = P["lam"]
        lt = (cx.sbuf([128, 256], F32), Buf())
        ls = (cx.sbuf([128, 4], F32), Buf())
        base = l * 512
        cx.op(cx.dve, lambda e: e.tensor_tensor(out=lt[0][:, 0:128], in0=lp[:, base:base + 128],
                                                in1=lp[:, base + 128:base + 256], op=ALU.mult), writes=[lt[1]])
        cx.op(cx.dve, lambda e: e.tensor_tensor(out=lt[0][:, 128:256], in0=lp[:, base + 256:base + 384],
                                                in1=lp[:, base + 384:base + 512], op=ALU.mult), writes=[lt[1]])
        cx.op(cx.dve, lambda e: e.reduce_sum(out=ls[0][:, 0:1], in_=lt[0][:, 0:128], axis=mybir.AxisListType.X),
              reads=[lt[1]], writes=[ls[1]])
        cx.op(cx.dve, lambda e: e.reduce_sum(out=ls[0][:, 1:2], in_=lt[0][:, 128:256], axis=mybir.AxisListType.X),
              reads=[lt[1]], writes=[ls[1]])
        cx.op(cx.act, lambda e: e.activation(out=ls[0][:, 0:2], in_=ls[0][:, 0:2], func=AF.Exp),
              reads=[ls[1]], writes=[ls[1]])
        cx.op(cx.dve, lambda e: e.tensor_tensor(out=ls[0][:, 2:3], in0=ls[0][:, 1:2], in1=ls[0][:, 0:1],
                                                op=ALU.subtract), reads=[ls[1]], writes=[ls[1]])
        cx.op(cx.dve, lambda e: e.tensor_scalar(out=ls[0][:, 3:4], in0=ls[0][:, 2:3], scalar1=-lam_init,
                                                scalar2=None, op0=ALU.add), reads=[ls[1]], writes=[ls[1]])
        neg_lam = ls[0][:, 3:4]
        sw = (cx.sbuf([128, 2], F32), Buf())
        cx.op(cx.dve, lambda e: e.tensor_scalar(out=sw[0][:], in0=P["subln"][:, 2 * l:2 * l + 2],
                                                scalar1=1.0 - lam_init, scalar2=None, op0=ALU.mult), writes=[sw[1]])
        vT = (cx.sbuf([128, 2, S], BF16), Buf())
        vtok = (cx.sbuf([128, KB, 256], BF16), Buf())
        kq = [[(cx.sbuf([128, S], BF16), Buf()) for _ in range(2)] for _ in range(2)]
        E = [(cx.sbuf([128, KB, QBW], BF16), Buf()) for _ in range(2)]
        sps = Rot([(cx.psum([128, 512], F32), Buf()) for _ in range(2)])
        pos = [[(cx.psum([128, 512], F32), Buf()) for _ in range(3)] for _ in range(2)]
        tps = sps
        rz = [(cx.sbuf([128, QBW], F32), Buf()) for _ in range(2)]
        ta = Rot([(cx.sbuf([128, QBW], F32), Buf()) for _ in range(2)])
        tb_ = Rot([(cx.sbuf([128, QBW], F32), Buf()) for _ in range(2)])
        od = [(cx.sbuf([128, QBW], F32), Buf()) for _ in range(2)]
        sq = Rot([(cx.sbuf([128, QBW], F32), Buf()) for _ in range(2)])
        rstd = (cx.sbuf([128, QBW], F32), Buf())
        ost = Rot([(cx.sbuf([128, QBW], BF16), Buf()) for _ in range(2)])
        import os as _os
        for h in range(min(HA, int(_os.environ.get('ATT_HEADS', HA)))):
            vr = 2 * WA + h * 256
            cx.dma(cx.sp, vT[0][:], pT[vr:vr + 256, :].rearrange("(t p) s -> p t s", p=128), writes=[vT[1]])
            for i in range(2):
                kr = WA + (2 * h + i) * 128
                qr = (2 * h + i) * 128
                cx.dma(cx.sp, kq[i][0][0][:], pT[kr:kr + 128, :], writes=[kq[i][0][1]])
                cx.dma(cx.sp, kq[i][1][0][:], pT[qr:qr + 128, :], writes=[kq[i][1][1]])
            for kb in range(KB):
                tp, tpb = tps.next()
                tpv = tp[:].bitcast(BF16)
                for t in range(2):
                    cx.op(cx.pe, lambda e: e.transpose(out=tpv[:, t * 128:(t + 1) * 128],
                                                       in_=vT[0][:, t, kb * 128:(kb + 1) * 128],
                                                       identity=T["ident"][:]),
                          reads=[vT[1]], writes=[tpb], inc=(t == 1))
                cx.op(cx.dve, lambda e: e.tensor_copy(out=vtok[0][:, kb, :], in_=tpv[:, 0:256]),
                      reads=[tpb], writes=[vtok[1]])
            def gen_S(qb, i):
                qs = slice(qb * QBW, (qb + 1) * QBW)
                for kb in range(KB):
                    sp_, spb = sps.next()
                    cx.op(cx.pe, lambda e: e.matmul(sp_[:, :QBW], lhsT=kq[i][0][0][:, kb * 128:(kb + 1) * 128],
                                                    rhs=kq[i][1][0][:, qs], start=True, stop=True),
                          reads=[kq[i][0][1], kq[i][1][1]], writes=[spb])
                    cx.op(cx.act, lambda e: e.activation(out=E[i][0][:, kb, :], in_=sp_[:, :QBW],
                                                         func=AF.Exp, scale=scale),
                          reads=[spb], writes=[E[i][1]])
                    yield

            def gen_PV(qb, i):
                for t in range(3):
                    po, pob = pos[i][t]
                    for kb in range(KB):
                        lhsT = vtok[0][:, kb, t * 128:(t + 1) * 128] if t < 2 else T["ones_bf"][:]
                        cx.op(cx.pe, lambda e: e.matmul(po[:, :QBW], lhsT=lhsT, rhs=E[i][0][:, kb, :],
                                                        start=(kb == 0), stop=(kb == KB - 1)),
                              reads=[vtok[1], E[i][1]], writes=[pob], inc=(kb == KB - 1))
                        yield

            def combine(qb):
                qs = slice(qb * QBW, (qb + 1) * QBW)
                for i in range(2):
                    cx.op(cx.dve, lambda e: e.reciprocal(out=rz[i][0][:], in_=pos[i][2][0][:, :QBW]),
                          reads=[pos[i][2][1]], writes=[rz[i][1]])
                for t in range(2):
                    a, ab = ta.next()
                    b_, bb = tb_.next()
                    cx.op(cx.dve, lambda e: e.tensor_tensor(out=a[:], in0=pos[0][t][0][:, :QBW], in1=rz[0][0][:],
                                                            op=ALU.mult), reads=[pos[0][t][1], rz[0][1]], writes=[ab])
                    cx.op(cx.dve, lambda e: e.tensor_tensor(out=b_[:], in0=pos[1][t][0][:, :QBW], in1=rz[1][0][:],
                                                            op=ALU.mult), reads=[pos[1][t][1], rz[1][1]], writes=[bb])
                    cx.op(cx.dve, lambda e: e.scalar_tensor_tensor(out=od[t][0][:], in0=b_[:], scalar=neg_lam,
                                                                   in1=a[:], op0=ALU.mult, op1=ALU.add),
                          reads=[ab, bb, ls[1]], writes=[od[t][1]])
                ssp, sspb = sps.next()
                for t in range(2):
                    s_, sb_ = sq.next()
                    cx.op(cx.act, lambda e: e.activation(out=s_[:], in_=od[t][0][:], func=AF.Square),
                          reads=[od[t][1]], writes=[sb_])
                    cx.op(cx.pe, lambda e: e.matmul(ssp[:, :QBW], lhsT=T["ones_f32"][:], rhs=s_[:],
                                                    start=(t == 0), stop=(t == 1)), reads=[sb_], writes=[sspb])
                cx.op(cx.act, lambda e: e.activation(out=rstd[0][:], in_=ssp[:, :QBW], func=AF.Sqrt,
                                                     bias=T["eps"][:], scale=1.0 / 256), reads=[sspb], writes=[rstd[1]])
                cx.op(cx.dve, lambda e: e.reciprocal(out=rstd[0][:], in_=rstd[0][:]), reads=[rstd[1]], writes=[rstd[1]])
                for t in range(2):
                    o_, ob = ost.next()
                    cx.op(cx.dve, lambda e: e.scalar_tensor_tensor(out=o_[:], in0=od[t][0][:], scalar=sw[0][:, t:t + 1],
                                                                   in1=rstd[0][:], op0=ALU.mult, op1=ALU.mult),
                          reads=[od[t][1], rstd[1], sw[1]], writes=[ob])
                    r0 = h * 256 + t * 128
                    cx.dma(cx.sp, aT[r0:r0 + 128, qs], o_[:], reads=[ob])

            units = [(qb, i) for qb in range(NQB) for i in range(2)]
            for _ in gen_S(*units[0]):
                pass
            for n, (qb, i) in enumerate(units):
                pv = gen_PV(qb, i)
                nx = gen_S(*units[n + 1]) if n + 1 < len(units) else None
                pv_alive, nx_alive = True, nx is not None
                while pv_alive or nx_alive:
                    if nx_alive:
                        try:
                            next(nx)
                        except StopIteration:
                            nx_alive = False
                    for _ in range(3):
                        if pv_alive:
                            try:
                                next(pv)
                            except StopIteration:
                                pv_alive = False
                if i == 1:
                    combine(qb)


def hgrn_phase(cx, cfg, l, T, P):
    S, WA, WB, HB, DEPTH = cfg.S, cfg.WA, cfg.WB, cfg.HB, cfg.DEPTH
    TBH = min(512, S)
    NBLK = S // TBH
    NCH = TBH // 64
    NK = TBH // 128
    G2 = 2 * HB
    pT = T["pT"].ap()
    sgT = T["sgT"].ap()
    bT = T["bT"].ap()
    X = mybir.AxisListType.X
    with Scope(cx):
        lg = P["lbl"]
        ex = (cx.sbuf([128, G2, DEPTH], F32), Buf())
        ssum = (cx.sbuf([128, G2], F32), Buf())
        lb = (cx.sbuf([128, G2], F32), Buf())
        oml = (cx.sbuf([128, G2], F32), Buf())
        noml = (cx.sbuf([128, G2], F32), Buf())
        cx.op(cx.act, lambda e: e.activation(out=ex[0][:], in_=lg.rearrange("p (g l) -> p g l", l=DEPTH), func=AF.Exp),
              writes=[ex[1]])
        cx.op(cx.dve, lambda e: e.reduce_sum(out=ssum[0][:], in_=ex[0][:], axis=X), reads=[ex[1]], writes=[ssum[1]])
        cx.op(cx.dve, lambda e: e.reciprocal(out=ssum[0][:], in_=ssum[0][:]), reads=[ssum[1]], writes=[ssum[1]])
        cx.op(cx.dve, lambda e: e.memset(lb[0][:], 0.0), writes=[lb[1]])
        for j in range(1, l + 1):
            cx.op(cx.dve, lambda e: e.tensor_tensor(out=lb[0][:], in0=lb[0][:], in1=ex[0][:, :, j], op=ALU.add),
                  reads=[ex[1], lb[1]], writes=[lb[1]])
        cx.op(cx.dve, lambda e: e.tensor_tensor(out=lb[0][:], in0=lb[0][:], in1=ssum[0][:], op=ALU.mult),
              reads=[ssum[1], lb[1]], writes=[lb[1]])
        cx.op(cx.dve, lambda e: e.tensor_scalar(out=oml[0][:], in0=lb[0][:], scalar1=-1.0, scalar2=1.0,
                                                op0=ALU.mult, op1=ALU.add), reads=[lb[1]], writes=[oml[1]])
        cx.op(cx.dve, lambda e: e.tensor_scalar(out=noml[0][:], in0=lb[0][:], scalar1=-1.0, scalar2=None,
                                                op0=ALU.add), reads=[lb[1]], writes=[noml[1]])
        ones_row = (cx.sbuf([128, TBH], BF16), Buf())
        cx.op(cx.dve, lambda e: e.memset(ones_row[0][:], 1.0), writes=[ones_row[1]])

        def mk(shape, dt):
            return (cx.sbuf(shape, dt), Buf())
        HP = 2 if HB % 2 == 0 else 1
        ch = []
        for d in range(2 * HP):
            ch.append(dict(
                sig=mk([128, TBH], F32), g=mk([128, TBH], F32), G=mk([128, TBH], F32), pex=mk([128, TBH], F32),
                ek=mk([128, TBH], F32), ex=mk([128, TBH], F32), kk=mk([128, TBH], BF16), Kh=mk([128, TBH], BF16),
                Qh=mk([128, TBH], BF16), q=mk([128, TBH], BF16), ib=mk([128, TBH], BF16),
                KV=mk([128, NK, 256], BF16), AT=mk([128, NK, 128], BF16),
                tot=mk([128, NCH], F32), dcc=mk([128, NCH], F32), S=mk([128, 128], F32), Sbf=mk([128, 128], BF16)))
        o_accs = [mk([128, S], F32) for _ in range(HP)]
        tpk = Rot([(cx.psum([128, 1024], BF16), Buf()) for _ in range(1)])
        scp = Rot([(cx.psum([128, 512], F32), Buf()) for _ in range(1)])
        pops = [(cx.psum([128, 512], F32), Buf()) for _ in range(2 * HP)]
        pstp = Rot([(cx.psum([128, 512], F32), Buf()) for _ in range(6 - 2 * HP)])
        fin_sq = mk([128, TBH], F32)
        fin_r = mk([128, TBH], F32)
        fin_t = mk([128, TBH], F32)
        fin_g = mk([128, TBH], BF16)
        fin_o = Rot([mk([128, TBH], BF16) for _ in range(2)])

        def prep(c, d, hb, j):
            t0 = j * TBH
            ts = slice(t0, t0 + TBH)
            gi = d * HB + hb
            cx.dma(cx.sp, c["sig"][0][:], sgT[d * WB + hb * 128:d * WB + hb * 128 + 128, ts], writes=[c["sig"][1]])
            cx.dma(cx.sp, c["q"][0][:], pT[3 * WA + hb * 128:3 * WA + hb * 128 + 128, ts], writes=[c["q"][1]])
            r_ib = 3 * WA + 3 * WB + hb * 128
            cx.dma(cx.sp, c["ib"][0][:], pT[r_ib:r_ib + 128, ts], writes=[c["ib"][1]])
            cx.op(cx.dve, lambda e: e.tensor_scalar(out=c["g"][0][:], in0=c["sig"][0][:], scalar1=oml[0][:, gi:gi + 1],
                                                    scalar2=lb[0][:, gi:gi + 1], op0=ALU.mult, op1=ALU.add),
                  reads=[c["sig"][1], oml[1], lb[1]], writes=[c["g"][1]])
            yield
            cx.op(cx.act, lambda e: e.activation(out=c["g"][0][:], in_=c["g"][0][:], func=AF.Ln),
                  reads=[c["g"][1]], writes=[c["g"][1]])
            yield
            cx.op(cx.dve, lambda e: e.tensor_scalar(out=c["kk"][0][:], in0=c["sig"][0][:], scalar1=noml[0][:, gi:gi + 1],
                                                    scalar2=oml[0][:, gi:gi + 1], op0=ALU.mult, op1=ALU.add),
                  reads=[c["sig"][1], oml[1], noml[1]], writes=[c["kk"][1]])
            yield
            cx.op(cx.dve, lambda e: e.tensor_tensor_scan(out=c["G"][0][:], data0=ones_row[0][:], data1=c["g"][0][:],
                                                         initial=0.0, op0=ALU.mult, op1=ALU.add),
                  reads=[c["g"][1], ones_row[1]], writes=[c["G"][1]])
            yield
            cx.op(cx.dve, lambda e: e.tensor_tensor(out=c["pex"][0][:], in0=c["G"][0][:], in1=c["g"][0][:],
                                                    op=ALU.subtract), reads=[c["G"][1], c["g"][1]], writes=[c["pex"][1]])
            yield
            G3 = c["G"][0][:].rearrange("p (c t) -> p c t", t=64)
            P3 = c["pex"][0][:].rearrange("p (c t) -> p c t", t=64)
            ek3 = c["ek"][0][:].rearrange("p (c t) -> p c t", t=64)
            if d == 0:
                ref = G3[:, :, 63:64].to_broadcast([128, NCH, 64])
                cx.op(cx.dve, lambda e: e.scalar_tensor_tensor(out=ek3, in0=G3, scalar=-1.0, in1=ref,
                                                               op0=ALU.mult, op1=ALU.add),
                      reads=[c["G"][1]], writes=[c["ek"][1]])
                yield
            else:
                ref = P3[:, :, 0:1].to_broadcast([128, NCH, 64])
                cx.op(cx.dve, lambda e: e.tensor_tensor(out=ek3, in0=P3, in1=ref, op=ALU.subtract),
                      reads=[c["pex"][1]], writes=[c["ek"][1]])
                yield
            cx.op(cx.dve, lambda e: e.tensor_tensor(out=c["tot"][0][:], in0=G3[:, :, 63], in1=P3[:, :, 0],
                                                    op=ALU.subtract), reads=[c["G"][1], c["pex"][1]], writes=[c["tot"][1]])
            yield
            cx.op(cx.act, lambda e: e.activation(out=c["dcc"][0][:], in_=c["tot"][0][:], func=AF.Exp),
                  reads=[c["tot"][1]], writes=[c["dcc"][1]])
            yield
            cx.op(cx.act, lambda e: e.activation(out=c["ex"][0][:], in_=c["ek"][0][:], func=AF.Exp, scale=-1.0),
                  reads=[c["ek"][1]], writes=[c["ex"][1]])
            yield
            cx.op(cx.dve, lambda e: e.tensor_tensor(out=c["Qh"][0][:], in0=c["q"][0][:], in1=c["ex"][0][:], op=ALU.mult),
                  reads=[c["q"][1], c["ex"][1]], writes=[c["Qh"][1]])
            yield
            cx.op(cx.act, lambda e: e.activation(out=c["ex"][0][:], in_=c["ek"][0][:], func=AF.Exp),
                  reads=[c["ek"][1]], writes=[c["ex"][1]])
            yield
            cx.op(cx.dve, lambda e: e.tensor_tensor(out=c["Kh"][0][:], in0=c["kk"][0][:], in1=c["ex"][0][:], op=ALU.mult),
                  reads=[c["kk"][1], c["ex"][1]], writes=[c["Kh"][1]])
            yield
            mask = T["mask_f"] if d == 0 else T["mask_b"]
            for kb in range(NK):
                ks = slice(kb * 128, (kb + 1) * 128)
                tp, tpb = tpk.next()
                tpv = tp[:]
                cx.op(cx.pe, lambda e: e.transpose(out=tpv[:, 0:128], in_=c["Kh"][0][:, ks], identity=T["ident"][:]),
                      reads=[c["Kh"][1]], writes=[tpb], inc=False)
                cx.op(cx.pe, lambda e: e.transpose(out=tpv[:, 128:256], in_=c["ib"][0][:, ks], identity=T["ident"][:]),
                      reads=[c["ib"][1]], writes=[tpb])
                if kb % 2 == 0:
                    cx.op(cx.act, lambda e: e.activation(out=c["KV"][0][:, kb, :], in_=tpv[:, 0:256], func=AF.Copy),
                          reads=[tpb], writes=[c["KV"][1]])
                else:
                    cx.op(cx.dve, lambda e: e.tensor_copy(out=c["KV"][0][:, kb, :], in_=tpv[:, 0:256]),
                          reads=[tpb], writes=[c["KV"][1]])
                sc, scb = scp.next()
                cx.op(cx.pe, lambda e: e.matmul(sc[:, 0:128], lhsT=c["Kh"][0][:, ks], rhs=c["Qh"][0][:, ks],
                                                start=True, stop=True), reads=[c["Kh"][1], c["Qh"][1]], writes=[scb])
                cx.op(cx.dve, lambda e: e.tensor_tensor(out=c["AT"][0][:, kb, :], in0=sc[:, 0:128], in1=mask[:],
                                                        op=ALU.mult), reads=[scb], writes=[c["AT"][1]])
                yield

        def seq(c, d, j, first_flags, written, o_acc):
            kbs = range(NK) if d == 0 else range(NK - 1, -1, -1)
            for kb in kbs:
                po, pob = pops[ch.index(c)]
                cx.op(cx.pe, lambda e: e.matmul(po[:, 0:128], lhsT=c["KV"][0][:, kb, 128:256], rhs=c["AT"][0][:, kb, :],
                                                start=True, stop=False), reads=[c["KV"][1], c["AT"][1]], writes=[pob])
                halves = (0, 1) if d == 0 else (1, 0)
                for hi, hf in enumerate(halves):
                    cidx = kb * 2 + hf
                    cols = slice(hf * 64, hf * 64 + 64)
                    toks = slice(kb * 128 + hf * 64, kb * 128 + hf * 64 + 64)
                    if first_flags[d]:
                        first_flags[d] = False
                        cx.op(cx.dve, lambda e: e.memset(c["S"][0][:], 0.0), writes=[c["S"][1]])
                    else:
                        cx.op(cx.dve, lambda e: e.tensor_scalar(out=c["S"][0][:], in0=c["S"][0][:],
                                                                scalar1=c["dcc"][0][:, cidx:cidx + 1], scalar2=None,
                                                                op0=ALU.mult),
                              reads=[c["S"][1], c["dcc"][1]], writes=[c["S"][1]])
                    cx.op(cx.act, lambda e: e.activation(out=c["Sbf"][0][:], in_=c["S"][0][:], func=AF.Copy),
                          reads=[c["S"][1]], writes=[c["Sbf"][1]])
                    cx.op(cx.pe, lambda e: e.matmul(po[:, cols], lhsT=c["Sbf"][0][:], rhs=c["Qh"][0][:, toks],
                                                    start=False, stop=(hi == 1)),
                          reads=[c["Sbf"][1], c["Qh"][1]], writes=[pob])
                    pst, pstb = pstp.next()
                    cx.op(cx.pe, lambda e: e.matmul(pst[:, 0:128], lhsT=c["KV"][0][hf * 64:hf * 64 + 64, kb, 0:128],
                                                    rhs=c["KV"][0][hf * 64:hf * 64 + 64, kb, 128:256], start=True, stop=True),
                          reads=[c["KV"][1]], writes=[pstb])
                    cx.op(cx.dve, lambda e: e.tensor_tensor(out=c["S"][0][:], in0=c["S"][0][:], in1=pst[:, 0:128],
                                                            op=ALU.add), reads=[c["S"][1], pstb], writes=[c["S"][1]])
                    yield
                og = slice(j * TBH + kb * 128, j * TBH + kb * 128 + 128)
                key = (j, kb)
                if key not in written:
                    written.add(key)
                    cx.op(cx.act, lambda e: e.activation(out=o_acc[0][:, og], in_=po[:, 0:128], func=AF.Copy),
                          reads=[pob], writes=[o_acc[1]])
                else:
                    cx.op(cx.dve, lambda e: e.tensor_tensor(out=o_acc[0][:, og], in0=o_acc[0][:, og], in1=po[:, 0:128],
                                                            op=ALU.add), reads=[pob, o_acc[1]], writes=[o_acc[1]])
                yield

        for hb0 in range(0, HB, HP):
            flags = [[True, True] for _ in range(HP)]
            writtens = [set() for _ in range(HP)]
            for step in range(NBLK):
                js = (step, NBLK - 1 - step)
                pgens = [prep(ch[hp * 2 + d], d, hb0 + hp, js[d]) for hp in range(HP) for d in range(2)]
                palive = [True] * len(pgens)
                while any(palive):
                    for gi in range(len(pgens)):
                        if palive[gi]:
                            try:
                                next(pgens[gi])
                            except StopIteration:
                                palive[gi] = False
                gens = [seq(ch[hp * 2 + d], d, js[d], flags[hp], writtens[hp], o_accs[hp])
                        for hp in range(HP) for d in range(2)]
                alive = [True] * len(gens)
                while any(alive):
                    for gi in range(len(gens)):
                        if alive[gi]:
                            try:
                                next(gens[gi])
                            except StopIteration:
                                alive[gi] = False
            for hp in range(HP):
              hb = hb0 + hp
              o_acc = o_accs[hp]
              r_g = 3 * WA + 4 * WB + hb * 128
              for j in range(NBLK):
                  ts = slice(j * TBH, (j + 1) * TBH)
                  cx.dma(cx.sp, fin_g[0][:], pT[r_g:r_g + 128, ts], writes=[fin_g[1]])
                  cx.op(cx.act, lambda e: e.activation(out=fin_sq[0][:], in_=o_acc[0][:, ts], func=AF.Square),
                        reads=[o_acc[1]], writes=[fin_sq[1]])
                  ps_, psb = scp.next()
                  cx.op(cx.pe, lambda e: e.matmul(ps_[:, :TBH], lhsT=T["ones_f32"][:], rhs=fin_sq[0][:], start=True, stop=True),
                        reads=[fin_sq[1]], writes=[psb])
                  cx.op(cx.act, lambda e: e.activation(out=fin_r[0][:], in_=ps_[:, :TBH], func=AF.Sqrt, bias=T["eps"][:],
                                                       scale=1.0 / 128), reads=[psb], writes=[fin_r[1]])
                  cx.op(cx.dve, lambda e: e.reciprocal(out=fin_r[0][:], in_=fin_r[0][:]), reads=[fin_r[1]], writes=[fin_r[1]])
                  cx.op(cx.dve, lambda e: e.scalar_tensor_tensor(out=fin_t[0][:], in0=o_acc[0][:, ts],
                                                                 scalar=P["hnw"][:, l:l + 1], in1=fin_r[0][:],
                                                                 op0=ALU.mult, op1=ALU.mult),
                        reads=[o_acc[1], fin_r[1]], writes=[fin_t[1]])
                  fo, fob = fin_o.next()
                  cx.op(cx.dve, lambda e: e.tensor_tensor(out=fo[:], in0=fin_t[0][:], in1=fin_g[0][:], op=ALU.mult),
                        reads=[fin_t[1], fin_g[1]], writes=[fob])
                  cx.dma(cx.sp, bT[hb * 128:hb * 128 + 128, ts], fo[:], reads=[fob])


def branch_phase(cx, cfg, l, T, which):
    D, S, WA, WB = cfg.D, cfg.S, cfg.WA, cfg.WB
    pT = T["pT"].ap()
    mA = T["mA"].ap()
    mT = T["mT"].ap()
    g0 = 3 * WA + 5 * WB + (0 if which == 0 else D)
    with Scope(cx):
        gt = Rot([(cx.sbuf([128, 512], BF16), Buf()) for _ in range(6)])
        at = Rot([(cx.sbuf([128, 512], F32), Buf()) for _ in range(6)])
        of = Rot([(cx.sbuf([128, 512], F32), Buf()) for _ in range(6)])
        ob = Rot([(cx.sbuf([128, 512], BF16), Buf()) for _ in range(6)])

        def epi(pt, pb, n0, t0, w):
            g, gb = gt.next()
            cx.dma(cx.act, g[:, :w], pT[g0 + n0:g0 + n0 + 128, t0:t0 + w], writes=[gb])
            if which == 0:
                o, obuf = of.next()
                cx.op(cx.dve, lambda e: e.tensor_tensor(out=o[:, :w], in0=pt[:, :w], in1=g[:, :w], op=ALU.mult),
                      reads=[pb, gb], writes=[obuf])
                cx.dma(cx.sp, mA[n0:n0 + 128, t0:t0 + w], o[:, :w], reads=[obuf])
            else:
                a, ab = at.next()
                cx.dma(cx.act, a[:, :w], mA[n0:n0 + 128, t0:t0 + w], writes=[ab])
                o, obuf = of.next()
                cx.op(cx.dve, lambda e: e.tensor_tensor(out=o[:, :w], in0=pt[:, :w], in1=g[:, :w], op=ALU.mult),
                      reads=[pb, gb], writes=[obuf])
                o2, o2b = ob.next()
                cx.op(cx.dve, lambda e: e.tensor_tensor(out=o2[:, :w], in0=o[:, :w], in1=a[:, :w], op=ALU.add),
                      reads=[obuf, ab], writes=[o2b])
                cx.dma(cx.sp, mT[n0:n0 + 128, t0:t0 + w], o2[:, :w], reads=[o2b])

        if which == 0:
            gemm_phase(cx, cfg, T["w_branch_a"].ap()[l], T["aT"].ap(), Buf(), WA, D, S, epi)
        else:
            gemm_phase(cx, cfg, T["w_branch_b"].ap()[l], T["bT"].ap(), Buf(), WB, D, S, epi)


def resid_gemm_phase(cx, cfg, W, xin, K, T):
    D, S = cfg.D, cfg.S
    xT = T["xT"].ap()
    with Scope(cx):
        depth = 6 if K <= 4096 else 3
        xt = Rot([(cx.sbuf([128, 512], F32), Buf()) for _ in range(depth)])
        ot = Rot([(cx.sbuf([128, 512], F32), Buf()) for _ in range(depth)])

        def epi(pt, pb, n0, t0, w):
            x_, xb = xt.next()
            cx.dma(cx.act, x_[:, :w], xT[n0:n0 + 128, t0:t0 + w], writes=[xb])
            o, obuf = ot.next()
            cx.op(cx.dve, lambda e: e.tensor_tensor(out=o[:, :w], in0=pt[:, :w], in1=x_[:, :w], op=ALU.add),
                  reads=[pb, xb], writes=[obuf])
            cx.dma(cx.sp, xT[n0:n0 + 128, t0:t0 + w], o[:, :w], reads=[obuf])

        gemm_phase(cx, cfg, W, xin, Buf(), K, D, S, epi)


def up_phase(cx, cfg, l, T):
    D, S, DFF = cfg.D, cfg.S, cfg.DFF
    uT = T["uT"].ap()
    with Scope(cx):
        st = Rot([(cx.sbuf([128, 512], BF16), Buf()) for _ in range(4)])
        evac = make_evac(cx)

        def epi(pt, pb, n0, t0, w):
            s_, sb_ = st.next()
            evac(s_[:, :w], pt[:, :w], [pb], [sb_])
            cx.dma(cx.sp, uT[n0:n0 + 128, t0:t0 + w], s_[:, :w], reads=[sb_])

        gemm_phase(cx, cfg, T["w_up"].ap()[l], T["hT"].ap(), Buf(), D, 2 * DFF, S, epi)


def conv_phase(cx, cfg, l, T, P):
    S, DFF = cfg.S, cfg.DFF
    NFC = 2 * DFF // 128
    NG = DFF // 128
    TW = min(512, S)
    uT = T["uT"].ap()
    actT = T["actT"].ap()
    cw, cb = P["convw"], P["convb"]
    with Scope(cx):
        ug = [(cx.sbuf([128, S + 2], BF16), Buf()) for _ in range(2)]
        uv = [(cx.sbuf([128, S + 2], BF16), Buf()) for _ in range(2)]
        dg = [(cx.sbuf([128, 6, 128], BF16), Buf()) for _ in range(2)]
        sg = Rot([(cx.sbuf([128, TW], F32), Buf()) for _ in range(3)])
        oo = [(cx.sbuf([128, S], BF16), Buf()) for _ in range(2)]
        pg = Rot([(cx.psum([128, 512], F32), Buf()) for _ in range(3)])
        pv = Rot([(cx.psum([128, 512], F32), Buf()) for _ in range(3)])
        for t_, b_ in ug + uv:
            cx.op(cx.dve, lambda e: e.memset(t_[:, 0:1], 0.0), writes=[b_])
            cx.op(cx.dve, lambda e: e.memset(t_[:, S + 1:S + 2], 0.0), writes=[b_])
        for c in range(NG):
            k = c % 2
            cx.dma(cx.sp, ug[k][0][:, 1:S + 1], uT[c * 128:(c + 1) * 128, :], writes=[ug[k][1]])
            cx.dma(cx.sp, uv[k][0][:, 1:S + 1], uT[DFF + c * 128:DFF + (c + 1) * 128, :], writes=[uv[k][1]])
            for j in range(3):
                for gi, cc in enumerate((c, NG + c)):
                    col = (l * 3 + j) * NFC + cc
                    cx.op(cx.dve, lambda e: e.tensor_scalar(out=dg[k][0][:, gi * 3 + j, :], in0=T["ident"],
                                                            scalar1=cw[:, col:col + 1], scalar2=None, op0=ALU.mult),
                          writes=[dg[k][1]])
            for tb in range(S // TW):
                a, ab = pg.next()
                v, vb = pv.next()
                for j in range(3):
                    cx.op(cx.pe, lambda e: e.matmul(a[:, :TW], lhsT=dg[k][0][:, j, :],
                                                    rhs=ug[k][0][:, tb * TW + j:tb * TW + j + TW],
                                                    start=(j == 0), stop=(j == 2)),
                          reads=[dg[k][1], ug[k][1]], writes=[ab], inc=(j == 2))
                for j in range(3):
                    cx.op(cx.pe, lambda e: e.matmul(v[:, :TW], lhsT=dg[k][0][:, 3 + j, :],
                                                    rhs=uv[k][0][:, tb * TW + j:tb * TW + j + TW],
                                                    start=(j == 0), stop=(j == 2)),
                          reads=[dg[k][1], uv[k][1]], writes=[vb], inc=(j == 2))
                s_, sb_ = sg.next()
                cx.op(cx.act, lambda e: e.activation(out=s_[:], in_=a[:, :TW], func=AF.Silu,
                                                     bias=cb[:, l * NFC + c:l * NFC + c + 1]),
                      reads=[ab], writes=[sb_])
                cx.op(cx.dve, lambda e: e.scalar_tensor_tensor(out=oo[k][0][:, tb * TW:(tb + 1) * TW], in0=v[:, :TW],
                                                               scalar=cb[:, l * NFC + NG + c:l * NFC + NG + c + 1],
                                                               in1=s_[:], op0=ALU.add, op1=ALU.mult),
                      reads=[vb, sb_], writes=[oo[k][1]])
            cx.dma(cx.sp, actT[c * 128:(c + 1) * 128, :], oo[k][0][:], reads=[oo[k][1]])


def param_layout(cfg):
    DC = cfg.D // 128
    NFC = 2 * cfg.DFF // 128
    segs = [("n1", cfg.DEPTH * DC), ("n2", cfg.DEPTH * DC), ("nf", DC), ("lam", cfg.DEPTH * 512),
            ("subln", cfg.DEPTH * 2), ("lbl", 2 * cfg.HB * cfg.DEPTH), ("hnw", cfg.DEPTH),
            ("convw", cfg.DEPTH * 3 * NFC), ("convb", cfg.DEPTH * NFC), ("mask_f", 128), ("mask_b", 128)]
    off, o = {}, 0
    for n, w in segs:
        off[n] = (o, o + w)
        o += w
    return off, o


def build(cfg, nlayers=None, stop_after=None):
    D, S, DEPTH = cfg.D, cfg.S, cfg.DEPTH
    nlayers = DEPTH if nlayers is None else nlayers
    nc = bass.Bass("TRN2", target_bir_lowering=False)
    T = {}

    def din(name, shape, dt=F32):
        T[name] = nc.dram_tensor(name, list(shape), dt, kind="ExternalInput")

    din("x_in", [D, S])
    din("w_in", [DEPTH, D, cfg.N_IN])
    din("w_branch_a", [DEPTH, cfg.WA, D])
    din("w_branch_b", [DEPTH, cfg.WB, D])
    din("w_out", [DEPTH, D, D])
    din("w_up", [DEPTH, D, 2 * cfg.DFF])
    din("w_down", [DEPTH, cfg.DFF, D])
    poff, NP = param_layout(cfg)
    din("params", [128, NP])
    din("c_cos", [128, S])
    din("c_sin", [128, S])
    din("c_bf", [128, 256], BF16)
    T["out"] = nc.dram_tensor("out", [D, S], F32, kind="ExternalOutput")
    for name, shape, dt in (("xT", [D, S], F32), ("hT", [D, S], BF16), ("pT", [cfg.N_IN, S], BF16),
                            ("sgT", [2 * cfg.WB, S], F32), ("aT", [cfg.WA, S], BF16), ("bT", [cfg.WB, S], BF16),
                            ("mA", [D, S], F32), ("mT", [D, S], BF16), ("uT", [2 * cfg.DFF, S], BF16),
                            ("actT", [cfg.DFF, S], BF16)):
        T[name] = nc.dram_tensor(name, shape, dt)
    cx = Ctx(nc)
    with cx.stack:
        par = cx.sbuf([128, NP], F32, "par")
        cbf = cx.sbuf([128, 256], BF16, "cbf")
        T["ones_f32"] = cx.sbuf([128, 128], F32, "ones_f32")
        T["ones_bf"] = cx.sbuf([128, 128], BF16, "ones_bf")
        T["eps"] = cx.sbuf([128, 1], F32, "eps")
        cb = Buf()
        cx.dma(cx.sp, par[:], T["params"].ap(), writes=[cb])
        cx.dma(cx.sp, cbf[:], T["c_bf"].ap(), writes=[cb])
        cx.op(cx.dve, lambda e: e.memset(T["ones_f32"][:], 1.0), writes=[Buf()])
        cx.op(cx.dve, lambda e: e.memset(T["ones_bf"][:], 1.0), writes=[Buf()])
        cx.op(cx.dve, lambda e: e.memset(T["eps"][:], cfg.EPS), writes=[Buf()])
        P = {n: par[:, a:b] for n, (a, b) in poff.items()}
        T["ident"] = cbf[:, 0:128]
        T["rot"] = cbf[:, 128:256]
        T["mask_f"] = P["mask_f"]
        T["mask_b"] = P["mask_b"]
        xi = T["x_in"].ap()
        xo = T["xT"].ap()
        nparts = 8
        rows = D // nparts
        for i in range(nparts):
            cx.dma(cx.sp, xo[i * rows:(i + 1) * rows, :], xi[i * rows:(i + 1) * rows, :])
        cx.barrier()
        DC = D // 128
        done = False
        for l in range(nlayers):
            rmsnorm_phase(cx, cfg, T["xT"].ap(), Buf(), P["n1"][:, l * DC:(l + 1) * DC], T["hT"].ap(), Buf(), T)
            proj_phase(cx, cfg, l, T)
            if stop_after == "proj":
                break
            attn_phase(cx, cfg, l, T, P)
            if stop_after == "attn":
                break
            hgrn_phase(cx, cfg, l, T, P)
            if stop_after == "hgrn":
                break
            branch_phase(cx, cfg, l, T, 0)
            branch_phase(cx, cfg, l, T, 1)
            resid_gemm_phase(cx, cfg, T["w_out"].ap()[l], T["mT"].ap(), D, T)
            if stop_after == "mixer":
                break
            rmsnorm_phase(cx, cfg, T["xT"].ap(), Buf(), P["n2"][:, l * DC:(l + 1) * DC], T["hT"].ap(), Buf(), T)
            up_phase(cx, cfg, l, T)
            conv_phase(cx, cfg, l, T, P)
            resid_gemm_phase(cx, cfg, T["w_down"].ap()[l], T["actT"].ap(), cfg.DFF, T)
        rmsnorm_phase(cx, cfg, T["xT"].ap(), Buf(), P["nf"], T["out"].ap(), Buf(), T, out_f32=True)
        cx.finish([])
    return nc, T


def host_consts(cfg):
    import ml_dtypes
    S = cfg.S
    inv = 1.0 / (10000.0 ** (np.arange(0, 128, 2, dtype=np.float32) / 128))
    ang = np.arange(S, dtype=np.float32)[:, None] * inv[None, :]
    cos = np.cos(ang).astype(np.float32).T
    sin = np.sin(ang).astype(np.float32).T
    c_cos = np.ascontiguousarray(np.concatenate([cos, cos], 0))
    c_sin = np.ascontiguousarray(np.concatenate([sin, sin], 0))
    ident = np.eye(128, dtype=np.float32)
    rot = np.zeros((128, 128), np.float32)
    for m in range(64):
        rot[m + 64, m] = -1.0
        rot[m, m + 64] = 1.0
    c_bf = np.concatenate([ident, rot], 1).astype(ml_dtypes.bfloat16)
    s_ = np.arange(128)[:, None]
    t_ = np.arange(128)[None, :]
    same = (s_ // 64) == (t_ // 64)
    mask_f = (same & (s_ <= t_)).astype(np.float32)
    mask_b = (same & (s_ >= t_)).astype(np.float32)
    return c_cos, c_sin, c_bf, mask_f, mask_b


def host_params(cfg, inp):
    D, DEPTH, HB = cfg.D, cfg.DEPTH, cfg.HB
    DC = D // 128
    NFC = 2 * cfg.DFF // 128
    poff, NP = param_layout(cfg)
    par = np.zeros((128, NP), np.float32)

    def put(name, arr):
        a, b = poff[name]
        par[:, a:b] = arr.reshape(128, b - a)

    f = lambda k: np.asarray(inp[k], np.float32)
    put("n1", f("norm1_w").reshape(DEPTH, DC, 128).transpose(2, 0, 1))
    put("n2", f("norm2_w").reshape(DEPTH, DC, 128).transpose(2, 0, 1))
    put("nf", f("final_norm_w").reshape(DC, 128).transpose(1, 0))
    put("lam", np.broadcast_to(f("diff_lambda").reshape(1, DEPTH * 512), (128, DEPTH * 512)))
    put("subln", f("diff_subln_w").reshape(DEPTH, 2, 128).transpose(2, 0, 1))
    put("lbl", f("hgrn_lb_logits").reshape(DEPTH, 2, HB, 128).transpose(3, 1, 2, 0))
    put("hnw", f("hgrn_norm_w").reshape(DEPTH, 128).transpose(1, 0))
    put("convw", f("conv_w").reshape(DEPTH, 3, NFC, 128).transpose(3, 0, 1, 2))
    put("convb", f("conv_b").reshape(DEPTH, NFC, 128).transpose(2, 0, 1))
    _, _, _, mask_f, mask_b = host_consts(cfg)
    put("mask_f", mask_f)
    put("mask_b", mask_b)
    return par


_CACHE = {}


def kernel(**inputs):
    x = np.asarray(inputs["x"], np.float32)
    B, S, D = x.shape
    DEPTH = np.asarray(inputs["w_in"]).shape[0]
    cfg = Cfg(D=D, S=S, DEPTH=DEPTH)
    key = (D, S, DEPTH)
    if key not in _CACHE:
        _CACHE[key] = build(cfg)
    nc, T = _CACHE[key]
    c_cos, c_sin, c_bf, _, _ = host_consts(cfg)
    par = host_params(cfg, inputs)
    ncores = B
    shared = {k: np.ascontiguousarray(np.asarray(inputs[k], np.float32)) for k in
              ("w_in", "w_branch_a", "w_branch_b", "w_out", "w_up", "w_down")}
    in_maps = []
    for c in range(ncores):
        b = c % B
        m = dict(shared)
        m["x_in"] = np.ascontiguousarray(x[b].T)
        m["params"] = par
        m["c_cos"] = c_cos
        m["c_sin"] = c_sin
        m["c_bf"] = c_bf
        in_maps.append(m)
    res = run_bass_kernel_spmd(nc, in_maps, core_ids=list(range(ncores)))
    out = np.stack([np.ascontiguousarray(res.results[b]["out"].T) for b in range(B)], 0)
    return out.astype(np.float32)
```

```python
from contextlib import ExitStack
import numpy as np
import concourse.bass as bass
import concourse.mybir as mybir
from concourse.bass_utils import run_bass_kernel_spmd

F32 = mybir.dt.float32
BF16 = mybir.dt.bfloat16
AF = mybir.ActivationFunctionType
ALU = mybir.AluOpType


class Buf:
    __slots__ = ("name", "w", "r")

    def __init__(self, name=""):
        self.name = name
        self.w = {}
        self.r = {}


class Q:
    def __init__(self, cx, eng, name, n_dma_sems=0):
        self.cx = cx
        self.e = eng
        self.name = name
        self.sem = cx.new_sem("q_" + name)
        self.count = 0
        self.seen = {}
        self.dsems = [[cx.new_sem(f"d_{name}{i}"), 0] for i in range(n_dma_sems)]
        self.dnext = 0


class Ctx:
    def __init__(self, nc):
        self.nc = nc
        self.stack = ExitStack()
        self.nsem = 0
        self.pe = Q(self, nc.tensor, "pe")
        self.act = Q(self, nc.scalar, "act", 12)
        self.dve = Q(self, nc.vector, "dve")
        self.pool = Q(self, nc.gpsimd, "pool", 24)
        self.sp = Q(self, nc.sync, "sp", 40)
        self.uid = 0

    def new_sem(self, name):
        self.nsem += 1
        return self.stack.enter_context(self.nc.semaphore(name))

    def sbuf(self, shape, dtype, name=None):
        self.uid += 1
        return self.stack.enter_context(
            self.nc.sbuf_tensor(name or f"sb{self.uid}", list(shape), dtype))

    def psum(self, shape, dtype, name=None):
        self.uid += 1
        return self.stack.enter_context(
            self.nc.psum_tensor(name or f"ps{self.uid}", list(shape), dtype))

    def _waits(self, q, reads, writes, extra=None):
        need = {}
        for b in reads:
            for s, v in b.w.items():
                if need.get(s, 0) < v:
                    need[s] = v
        for b in writes:
            for d in (b.w, b.r):
                for s, v in d.items():
                    if need.get(s, 0) < v:
                        need[s] = v
        if extra:
            for s, v in extra:
                if need.get(s, 0) < v:
                    need[s] = v
        for s, v in need.items():
            if s is q.sem and q is self.pe:
                continue
            if q.seen.get(s, 0) >= v:
                continue
            q.e.wait_ge(s, v)
            q.seen[s] = v

    def op(self, q, fn, reads=(), writes=(), inc=True):
        self._waits(q, reads, writes)
        ins = fn(q.e)
        val = q.count + 1
        if inc:
            ins.then_inc(q.sem, 1)
            q.count = val
        for b in writes:
            b.w = {q.sem: val}
            b.r = {}
        for b in reads:
            if b.r.get(q.sem, 0) < val:
                b.r[q.sem] = val
        return ins

    def dma(self, q, out, in_, reads=(), writes=(), **kw):
        slot = q.dsems[q.dnext]
        q.dnext = (q.dnext + 1) % len(q.dsems)
        sem, cnt = slot
        self._waits(q, reads, writes, extra=[(sem, cnt)] if cnt else None)
        ins = q.e.dma_start(out=out, in_=in_, **kw)
        cnt += 16
        slot[1] = cnt
        ins.then_inc(sem, 16)
        for b in writes:
            b.w = {sem: cnt}
            b.r = {}
        for b in reads:
            b.r[sem] = cnt
        return ins

    def barrier(self):
        need = {}
        qs = (self.pe, self.act, self.dve, self.pool, self.sp)
        for qq in qs:
            if qq.count:
                need[qq.sem] = qq.count
            for s, c in qq.dsems:
                if c:
                    need[s] = c
        for q in qs:
            for s, v in need.items():
                if s is q.sem:
                    continue
                if q.seen.get(s, 0) < v:
                    q.e.wait_ge(s, v)
                    q.seen[s] = v

    def finish(self, bufs):
        q = self.sp
        need = {}
        for qq in (self.pe, self.act, self.dve, self.pool, self.sp):
            if qq.count:
                need[qq.sem] = qq.count
            for s, c in qq.dsems:
                if c:
                    need[s] = c
        for s, v in need.items():
            if q.seen.get(s, 0) < v:
                q.e.wait_ge(s, v)
                q.seen[s] = v


class Rot:
    def __init__(self, items):
        self.items = items
        self.i = 0

    def next(self):
        it = self.items[self.i]
        self.i = (self.i + 1) % len(self.items)
        return it


class Cfg:
    def __init__(self, D=4096, S=4096, DEPTH=4):
        self.D = D
        self.S = S
        self.DEPTH = DEPTH
        self.WA = D // 2
        self.HDA = 128
        self.HA = self.WA // 256
        self.WB = D // 2
        self.HB = self.WB // 128
        self.DFF = 7 * D // 4
        self.N_IN = 3 * self.WA + 5 * self.WB + 2 * D
        self.EPS = 1e-6
        self.CHUNK = 64


def rmsnorm_phase(cx, cfg, x_dram, xbuf, w_sb, h_dram, hbuf, consts, out_f32=False):
    nc = cx.nc
    D, S = cfg.D, cfg.S
    DC = D // 128
    TBK = min(256, S)
    xv = x_dram.rearrange("(c p) s -> p c s", p=128)
    hv = h_dram.rearrange("(c p) s -> p c s", p=128)
    with ExitStack() as st:
        old = cx.stack
        cx.stack = st
        xt = Rot([(cx.sbuf([128, DC, TBK], F32), Buf()) for _ in range(2)])
        ht = Rot([(cx.sbuf([128, DC, TBK], F32 if out_f32 else BF16), Buf()) for _ in range(2)])
        sq = Rot([(cx.sbuf([128, TBK], F32), Buf()) for _ in range(3)])
        rstd = (cx.sbuf([128, TBK], F32), Buf())
        ps = (cx.psum([128, TBK], F32), Buf())
        NB_ = S // TBK
        nxt_x = xt.next()
        cx.dma(cx.sp, nxt_x[0][:], xv[:, :, 0:TBK], reads=[xbuf], writes=[nxt_x[1]])
        for tb in range(NB_):
            xs, xb = nxt_x
            hs, hb = ht.next()
            sl = slice(tb * TBK, (tb + 1) * TBK)
            if tb + 1 < NB_:
                nxt_x = xt.next()
                cx.dma(cx.sp, nxt_x[0][:], xv[:, :, (tb + 1) * TBK:(tb + 2) * TBK], reads=[xbuf], writes=[nxt_x[1]])
            for c in range(DC):
                sqs, sqb = sq.next()
                cx.op(cx.act, lambda e: e.activation(out=sqs[:], in_=xs[:, c, :], func=AF.Square),
                      reads=[xb], writes=[sqb])
                cx.op(cx.pe, lambda e: e.matmul(ps[0][:], lhsT=consts["ones_f32"][:], rhs=sqs[:],
                                                start=(c == 0), stop=(c == DC - 1)),
                      reads=[sqb], writes=[ps[1]])
            cx.op(cx.act, lambda e: e.activation(out=rstd[0][:], in_=ps[0][:], func=AF.Sqrt,
                                                 bias=consts["eps"][:], scale=1.0 / D),
                  reads=[ps[1]], writes=[rstd[1]])
            cx.op(cx.dve, lambda e: e.reciprocal(out=rstd[0][:], in_=rstd[0][:]),
                  reads=[rstd[1]], writes=[rstd[1]])
            for c in range(DC):
                cx.op(cx.dve, lambda e: e.scalar_tensor_tensor(
                    out=hs[:, c, :], in0=xs[:, c, :], scalar=w_sb[:, c:c + 1], in1=rstd[0][:],
                    op0=ALU.mult, op1=ALU.mult),
                    reads=[xb, rstd[1]], writes=[hb])
            cx.dma(cx.sp, hv[:, :, sl], hs[:], reads=[hb], writes=[hbuf])
        cx.barrier()
        cx.stack = old


def gemm_phase(cx, cfg, W, xin, xin_buf, K, N, S, epilogue, TB=None, CW=512):
    KC = K // 128
    if TB is None:
        TB = 1024 if KC <= 32 else 512
    TB = min(TB, S)
    CW = min(CW, N)
    while N % CW:
        CW //= 2
    SBK = min(512, TB)
    Wv = W.rearrange("(kc p) n -> p kc n", p=128)
    Xv = xin.rearrange("(kc p) s -> p kc s", p=128)
    with ExitStack() as st:
        old = cx.stack
        cx.stack = st
        xblk = (cx.sbuf([128, KC, TB], BF16), Buf())
        wsl = Rot([(cx.sbuf([128, KC, CW], BF16), Buf()) for _ in range(2)])
        banks = Rot([(cx.psum([128, 512], F32), Buf()) for _ in range(4)])
        NG = N // CW
        for tb in range(S // TB):
            cx.dma(cx.sp, xblk[0][:], Xv[:, :, tb * TB:(tb + 1) * TB], reads=[xin_buf], writes=[xblk[1]])
            cur = wsl.next()
            cx.dma(cx.pool, cur[0][:], Wv[:, :, 0:CW], writes=[cur[1]])
            for ng in range(NG):
                nxt = None
                if ng + 1 < NG:
                    nxt = wsl.next()
                    cx.dma(cx.pool, nxt[0][:], Wv[:, :, (ng + 1) * CW:(ng + 2) * CW], writes=[nxt[1]])
                ws, wb = cur
                for nt in range(CW // 128):
                    for sb in range(TB // SBK):
                        pt, pb = banks.next()
                        for kc in range(KC):
                            cx.op(cx.pe, lambda e: e.matmul(
                                pt[:, :SBK], lhsT=ws[:, kc, nt * 128:(nt + 1) * 128],
                                rhs=xblk[0][:, kc, sb * SBK:(sb + 1) * SBK],
                                start=(kc == 0), stop=(kc == KC - 1)),
                                reads=[wb, xblk[1]], writes=[pb], inc=(kc == KC - 1))
                        epilogue(pt, pb, ng * CW + nt * 128, tb * TB + sb * SBK, SBK)
                cur = nxt
        cx.barrier()
        cx.stack = old


class Scope:
    def __init__(self, cx):
        self.cx = cx

    def __enter__(self):
        self.st = ExitStack()
        self.st.__enter__()
        self.old = self.cx.stack
        self.cx.stack = self.st
        return self

    def __exit__(self, *a):
        self.cx.barrier()
        self.cx.stack = self.old
        return self.st.__exit__(*a)


def make_evac(cx):
    flip = [0]

    def evac(out, in_, reads, writes, func=None, scale=1.0):
        flip[0] ^= 1
        if func is not None or flip[0]:
            f = func if func is not None else AF.Copy
            cx.op(cx.act, lambda e: e.activation(out=out, in_=in_, func=f, scale=scale),
                  reads=reads, writes=writes)
        else:
            cx.op(cx.dve, lambda e: e.tensor_copy(out=out, in_=in_), reads=reads, writes=writes)
    return evac


def proj_phase(cx, cfg, l, T):
    D, S, WA, WB = cfg.D, cfg.S, cfg.WA, cfg.WB
    segs = []
    o = 0
    for name, w in (("qa", WA), ("ka", WA), ("va", WA), ("qb", WB), ("ff", WB), ("fb", WB),
                    ("ib", WB), ("gb", WB), ("ga", D), ("gg", D)):
        segs.append((o, o + w, name))
        o += w
    pT = T["pT"].ap()
    sgT = T["sgT"].ap()
    cosv, sinv = T["c_cos"].ap(), T["c_sin"].ap()
    with Scope(cx):
        st_bf = Rot([(cx.sbuf([128, 512], BF16), Buf()) for _ in range(4)])
        st_f = Rot([(cx.sbuf([128, 512], F32), Buf()) for _ in range(3)])
        xr = Rot([(cx.sbuf([128, 512], BF16), Buf()) for _ in range(2)])
        t1 = Rot([(cx.sbuf([128, 512], F32), Buf()) for _ in range(2)])
        t2 = Rot([(cx.sbuf([128, 512], F32), Buf()) for _ in range(2)])
        CSW = min(1024, S)
        cs = [(cx.sbuf([128, CSW], F32), Buf()), (cx.sbuf([128, CSW], F32), Buf())]
        rps = (cx.psum([128, 512], F32), Buf())
        cs_t0 = [-1]

        def epi(pt, pb, n0, t0, w):
            kind = [k for (a, b, k) in segs if a <= n0 < b][0]
            import os as _os
            dbg = _os.environ.get("PROJ_DBG", "")
            if dbg == "simple":
                kind = "va"
            elif dbg == "norope" and kind in ("qa", "ka"):
                kind = "va"
            elif dbg == "nosg" and kind in ("ff", "fb"):
                kind = "va"
            if kind in ("qa", "ka"):
                cblk = t0 // CSW
                co = t0 % CSW
                if cs_t0[0] != cblk:
                    cs_t0[0] = cblk
                    cx.dma(cx.sp, cs[0][0][:], cosv[:, cblk * CSW:(cblk + 1) * CSW], writes=[cs[0][1]])
                    cx.dma(cx.sp, cs[1][0][:], sinv[:, cblk * CSW:(cblk + 1) * CSW], writes=[cs[1][1]])
                xs, xb = xr.next()
                cx.op(cx.act, lambda e: e.activation(out=xs[:, :w], in_=pt[:, :w], func=AF.Copy),
                      reads=[pb], writes=[xb])
                cx.op(cx.pe, lambda e: e.matmul(rps[0][:, :w], lhsT=T["rot"][:], rhs=xs[:, :w],
                                                start=True, stop=True), reads=[xb], writes=[rps[1]])
                a, ab = t1.next()
                cx.op(cx.dve, lambda e: e.tensor_tensor(out=a[:, :w], in0=xs[:, :w], in1=cs[0][0][:, co:co + w],
                                                        op=ALU.mult), reads=[xb, cs[0][1]], writes=[ab])
                b_, bb = t2.next()
                cx.op(cx.dve, lambda e: e.tensor_tensor(out=b_[:, :w], in0=rps[0][:, :w], in1=cs[1][0][:, co:co + w],
                                                        op=ALU.mult), reads=[rps[1], cs[1][1]], writes=[bb])
                ss, sb_ = st_bf.next()
                cx.op(cx.dve, lambda e: e.tensor_tensor(out=ss[:, :w], in0=a[:, :w], in1=b_[:, :w], op=ALU.add),
                      reads=[ab, bb], writes=[sb_])
                cx.dma(cx.sp, pT[n0:n0 + 128, t0:t0 + w], ss[:, :w], reads=[sb_])
            elif kind in ("ff", "fb"):
                ss, sb_ = st_f.next()
                cx.op(cx.act, lambda e: e.activation(out=ss[:, :w], in_=pt[:, :w], func=AF.Sigmoid),
                      reads=[pb], writes=[sb_])
                r0 = n0 - segs[4][0]
                cx.dma(cx.sp, sgT[r0:r0 + 128, t0:t0 + w], ss[:, :w], reads=[sb_])
            else:
                f = {"va": AF.Copy, "ib": AF.Copy, "qb": AF.Silu, "gb": AF.Silu,
                     "ga": AF.Sigmoid, "gg": AF.Sigmoid}[kind]
                ss, sb_ = st_bf.next()
                if f == AF.Copy and (n0 // 128) % 2:
                    cx.op(cx.dve, lambda e: e.tensor_copy(out=ss[:, :w], in_=pt[:, :w]), reads=[pb], writes=[sb_])
                else:
                    cx.op(cx.act, lambda e: e.activation(out=ss[:, :w], in_=pt[:, :w], func=f),
                          reads=[pb], writes=[sb_])
                cx.dma(cx.sp, pT[n0:n0 + 128, t0:t0 + w], ss[:, :w], reads=[sb_])

        gemm_phase(cx, cfg, T["w_in"].ap()[l], T["hT"].ap(), Buf(), D, cfg.N_IN, S, epi)


def attn_phase(cx, cfg, l, T, P):
    S, WA, HA = cfg.S, cfg.WA, cfg.HA
    KB = S // 128
    QBW = min(512, S)
    NQB = S // QBW
    pT = T["pT"].ap()
    aT = T["aT"].ap()
    scale = 128 ** -0.5
    lam_init = 0.8 - 0.6 * float(np.exp(-0.3 * l))
    with Scope(cx):
        lp = P["lam"]
        lt = (cx.sbuf([128, 256], F32), Buf())
        ls = (cx.sbuf([128, 4], F32), Buf())
        base = l * 512
        cx.op(cx.dve, lambda e: e.tensor_tensor(out=lt[0][:, 0:128], in0=lp[:, base:base + 128],
                                                in1=lp[:, base + 128:base + 256], op=ALU.mult), writes=[lt[1]])
        cx.op(cx.dve, lambda e: e.tensor_tensor(out=lt[0][:, 128:256], in0=lp[:, base + 256:base + 384],
                                                in1=lp[:, base + 384:base + 512], op=ALU.mult), writes=[lt[1]])
        cx.op(cx.dve, lambda e: e.reduce_sum(out=ls[0][:, 0:1], in_=lt[0][:, 0:128], axis=mybir.AxisListType.X),
              reads=[lt[1]], writes=[ls[1]])
        cx.op(cx.dve, lambda e: e.reduce_sum(out=ls[0][:, 1:2], in_=lt[0][:, 128:256], axis=mybir.AxisListType.X),
              reads=[lt[1]], writes=[ls[1]])
        cx.op(cx.act, lambda e: e.activation(out=ls[0][:, 0:2], in_=ls[0][:, 0:2], func=AF.Exp),
              reads=[ls[1]], writes=[ls[1]])
        cx.op(cx.dve, lambda e: e.tensor_tensor(out=ls[0][:, 2:3], in0=ls[0][:, 1:2], in1=ls[0][:, 0:1],
                                                op=ALU.subtract), reads=[ls[1]], writes=[ls[1]])
        cx.op(cx.dve, lambda e: e.tensor_scalar(out=ls[0][:, 3:4], in0=ls[0][:, 2:3], scalar1=-lam_init,
                                                scalar2=None, op0=ALU.add), reads=[ls[1]], writes=[ls[1]])
        neg_lam = ls[0][:, 3:4]
        sw = (cx.sbuf([128, 2], F32), Buf())
        cx.op(cx.dve, lambda e: e.tensor_scalar(out=sw[0][:], in0=P["subln"][:, 2 * l:2 * l + 2],
                                                scalar1=1.0 - lam_init, scalar2=None, op0=ALU.mult), writes=[sw[1]])
        vT = (cx.sbuf([128, 2, S], BF16), Buf())
        vtok = (cx.sbuf([128, KB, 256], BF16), Buf())
        kq = [[(cx.sbuf([128, S], BF16), Buf()) for _ in range(2)] for _ in range(2)]
        E = [(cx.sbuf([128, KB, QBW], BF16), Buf()) for _ in range(2)]
        sps = Rot([(cx.psum([128, 512], F32), Buf()) for _ in range(2)])
        pos = [[(cx.psum([128, 512], F32), Buf()) for _ in range(3)] for _ in range(2)]
        tps = sps
        rz = [(cx.sbuf([128, QBW], F32), Buf()) for _ in range(2)]
        ta = Rot([(cx.sbuf([128, QBW], F32), Buf()) for _ in range(2)])
        tb_ = Rot([(cx.sbuf([128, QBW], F32), Buf()) for _ in range(2)])
        od = [(cx.sbuf([128, QBW], F32), Buf()) for _ in range(2)]
        sq = Rot([(cx.sbuf([128, QBW], F32), Buf()) for _ in range(2)])
        rstd = (cx.sbuf([128, QBW], F32), Buf())
        ost = Rot([(cx.sbuf([128, QBW], BF16), Buf()) for _ in range(2)])
        import os as _os
        for h in range(min(HA, int(_os.environ.get('ATT_HEADS', HA)))):
            vr = 2 * WA + h * 256
            cx.dma(cx.sp, vT[0][:], pT[vr:vr + 256, :].rearrange("(t p) s -> p t s", p=128), writes=[vT[1]])
            for i in range(2):
                kr = WA + (2 * h + i) * 128
                qr = (2 * h + i) * 128
                cx.dma(cx.sp, kq[i][0][0][:], pT[kr:kr + 128, :], writes=[kq[i][0][1]])
                cx.dma(cx.sp, kq[i][1][0][:], pT[qr:qr + 128, :], writes=[kq[i][1][1]])
            for kb in range(KB):
                tp, tpb = tps.next()
                tpv = tp[:].bitcast(BF16)
                for t in range(2):
                    cx.op(cx.pe, lambda e: e.transpose(out=tpv[:, t * 128:(t + 1) * 128],
                                                       in_=vT[0][:, t, kb * 128:(kb + 1) * 128],
                                                       identity=T["ident"][:]),
                          reads=[vT[1]], writes=[tpb], inc=(t == 1))
                cx.op(cx.dve, lambda e: e.tensor_copy(out=vtok[0][:, kb, :], in_=tpv[:, 0:256]),
                      reads=[tpb], writes=[vtok[1]])
            def gen_S(qb, i):
                qs = slice(qb * QBW, (qb + 1) * QBW)
                for kb in range(KB):
                    sp_, spb = sps.next()
                    cx.op(cx.pe, lambda e: e.matmul(sp_[:, :QBW], lhsT=kq[i][0][0][:, kb * 128:(kb + 1) * 128],
                                                    rhs=kq[i][1][0][:, qs], start=True, stop=True),
                          reads=[kq[i][0][1], kq[i][1][1]], writes=[spb])
                    cx.op(cx.act, lambda e: e.activation(out=E[i][0][:, kb, :], in_=sp_[:, :QBW],
                                                         func=AF.Exp, scale=scale),
                          reads=[spb], writes=[E[i][1]])
                    yield

            def gen_PV(qb, i):
                for t in range(3):
                    po, pob = pos[i][t]
                    for kb in range(KB):
                        lhsT = vtok[0][:, kb, t * 128:(t + 1) * 128] if t < 2 else T["ones_bf"][:]
                        cx.op(cx.pe, lambda e: e.matmul(po[:, :QBW], lhsT=lhsT, rhs=E[i][0][:, kb, :],
                                                        start=(kb == 0), stop=(kb == KB - 1)),
                              reads=[vtok[1], E[i][1]], writes=[pob], inc=(kb == KB - 1))
                        yield

            def combine(qb):
                qs = slice(qb * QBW, (qb + 1) * QBW)
                for i in range(2):
                    cx.op(cx.dve, lambda e: e.reciprocal(out=rz[i][0][:], in_=pos[i][2][0][:, :QBW]),
                          reads=[pos[i][2][1]], writes=[rz[i][1]])
                for t in range(2):
                    a, ab = ta.next()
                    b_, bb = tb_.next()
                    cx.op(cx.dve, lambda e: e.tensor_tensor(out=a[:], in0=pos[0][t][0][:, :QBW], in1=rz[0][0][:],
                                                            op=ALU.mult), reads=[pos[0][t][1], rz[0][1]], writes=[ab])
                    cx.op(cx.dve, lambda e: e.tensor_tensor(out=b_[:], in0=pos[1][t][0][:, :QBW], in1=rz[1][0][:],
                                                            op=ALU.mult), reads=[pos[1][t][1], rz[1][1]], writes=[bb])
                    cx.op(cx.dve, lambda e: e.scalar_tensor_tensor(out=od[t][0][:], in0=b_[:], scalar=neg_lam,
                                                                   in1=a[:], op0=ALU.mult, op1=ALU.add),
                          reads=[ab, bb, ls[1]], writes=[od[t][1]])
                ssp, sspb = sps.next()
                for t in range(2):
                    s_, sb_ = sq.next()
                    cx.op(cx.act, lambda e: e.activation(out=s_[:], in_=od[t][0][:], func=AF.Square),
                          reads=[od[t][1]], writes=[sb_])
                    cx.op(cx.pe, lambda e: e.matmul(ssp[:, :QBW], lhsT=T["ones_f32"][:], rhs=s_[:],
                                                    start=(t == 0), stop=(t == 1)), reads=[sb_], writes=[sspb])
                cx.op(cx.act, lambda e: e.activation(out=rstd[0][:], in_=ssp[:, :QBW], func=AF.Sqrt,
                                                     bias=T["eps"][:], scale=1.0 / 256), reads=[sspb], writes=[rstd[1]])
                cx.op(cx.dve, lambda e: e.reciprocal(out=rstd[0][:], in_=rstd[0][:]), reads=[rstd[1]], writes=[rstd[1]])
                for t in range(2):
                    o_, ob = ost.next()
                    cx.op(cx.dve, lambda e: e.scalar_tensor_tensor(out=o_[:], in0=od[t][0][:], scalar=sw[0][:, t:t + 1],
                                                                   in1=rstd[0][:], op0=ALU.mult, op1=ALU.mult),
                          reads=[od[t][1], rstd[1], sw[1]], writes=[ob])
                    r0 = h * 256 + t * 128
                    cx.dma(cx.sp, aT[r0:r0 + 128, qs], o_[:], reads=[ob])

            units = [(qb, i) for qb in range(NQB) for i in range(2)]
            for _ in gen_S(*units[0]):
                pass
            for n, (qb, i) in enumerate(units):
                pv = gen_PV(qb, i)
                nx = gen_S(*units[n + 1]) if n + 1 < len(units) else None
                pv_alive, nx_alive = True, nx is not None
                while pv_alive or nx_alive:
                    if nx_alive:
                        try:
                            next(nx)
                        except StopIteration:
                            nx_alive = False
                    for _ in range(3):
                        if pv_alive:
                            try:
                                next(pv)
                            except StopIteration:
                                pv_alive = False
                if i == 1:
                    combine(qb)


def hgrn_phase(cx, cfg, l, T, P):
    S, WA, WB, HB, DEPTH = cfg.S, cfg.WA, cfg.WB, cfg.HB, cfg.DEPTH
    TBH = min(512, S)
    NBLK = S // TBH
    NCH = TBH // 64
    NK = TBH // 128
    G2 = 2 * HB
    pT = T["pT"].ap()
    sgT = T["sgT"].ap()
    bT = T["bT"].ap()
    X = mybir.AxisListType.X
    with Scope(cx):
        lg = P["lbl"]
        ex = (cx.sbuf([128, G2, DEPTH], F32), Buf())
        ssum = (cx.sbuf([128, G2], F32), Buf())
        lb = (cx.sbuf([128, G2], F32), Buf())
        oml = (cx.sbuf([128, G2], F32), Buf())
        noml = (cx.sbuf([128, G2], F32), Buf())
        cx.op(cx.act, lambda e: e.activation(out=ex[0][:], in_=lg.rearrange("p (g l) -> p g l", l=DEPTH), func=AF.Exp),
              writes=[ex[1]])
        cx.op(cx.dve, lambda e: e.reduce_sum(out=ssum[0][:], in_=ex[0][:], axis=X), reads=[ex[1]], writes=[ssum[1]])
        cx.op(cx.dve, lambda e: e.reciprocal(out=ssum[0][:], in_=ssum[0][:]), reads=[ssum[1]], writes=[ssum[1]])
        cx.op(cx.dve, lambda e: e.memset(lb[0][:], 0.0), writes=[lb[1]])
        for j in range(1, l + 1):
            cx.op(cx.dve, lambda e: e.tensor_tensor(out=lb[0][:], in0=lb[0][:], in1=ex[0][:, :, j], op=ALU.add),
                  reads=[ex[1], lb[1]], writes=[lb[1]])
        cx.op(cx.dve, lambda e: e.tensor_tensor(out=lb[0][:], in0=lb[0][:], in1=ssum[0][:], op=ALU.mult),
              reads=[ssum[1], lb[1]], writes=[lb[1]])
        cx.op(cx.dve, lambda e: e.tensor_scalar(out=oml[0][:], in0=lb[0][:], scalar1=-1.0, scalar2=1.0,
                                                op0=ALU.mult, op1=ALU.add), reads=[lb[1]], writes=[oml[1]])
        cx.op(cx.dve, lambda e: e.tensor_scalar(out=noml[0][:], in0=lb[0][:], scalar1=-1.0, scalar2=None,
                                                op0=ALU.add), reads=[lb[1]], writes=[noml[1]])
        ones_row = (cx.sbuf([128, TBH], BF16), Buf())
        cx.op(cx.dve, lambda e: e.memset(ones_row[0][:], 1.0), writes=[ones_row[1]])

        def mk(shape, dt):
            return (cx.sbuf(shape, dt), Buf())
        HP = 2 if HB % 2 == 0 else 1
        ch = []
        for d in range(2 * HP):
            ch.append(dict(
                sig=mk([128, TBH], F32), g=mk([128, TBH], F32), G=mk([128, TBH], F32), pex=mk([128, TBH], F32),
                ek=mk([128, TBH], F32), ex=mk([128, TBH], F32), kk=mk([128, TBH], BF16), Kh=mk([128, TBH], BF16),
                Qh=mk([128, TBH], BF16), q=mk([128, TBH], BF16), ib=mk([128, TBH], BF16),
                KV=mk([128, NK, 256], BF16), AT=mk([128, NK, 128], BF16),
                tot=mk([128, NCH], F32), dcc=mk([128, NCH], F32), S=mk([128, 128], F32), Sbf=mk([128, 128], BF16)))
        o_accs = [mk([128, S], F32) for _ in range(HP)]
        tpk = Rot([(cx.psum([128, 1024], BF16), Buf()) for _ in range(1)])
        scp = Rot([(cx.psum([128, 512], F32), Buf()) for _ in range(1)])
        pops = [(cx.psum([128, 512], F32), Buf()) for _ in range(2 * HP)]
        pstp = Rot([(cx.psum([128, 512], F32), Buf()) for _ in range(6 - 2 * HP)])
        fin_sq = mk([128, TBH], F32)
        fin_r = mk([128, TBH], F32)
        fin_t = mk([128, TBH], F32)
        fin_g = mk([128, TBH], BF16)
        fin_o = Rot([mk([128, TBH], BF16) for _ in range(2)])

        def prep(c, d, hb, j):
            t0 = j * TBH
            ts = slice(t0, t0 + TBH)
            gi = d * HB + hb
            cx.dma(cx.sp, c["sig"][0][:], sgT[d * WB + hb * 128:d * WB + hb * 128 + 128, ts], writes=[c["sig"][1]])
            cx.dma(cx.sp, c["q"][0][:], pT[3 * WA + hb * 128:3 * WA + hb * 128 + 128, ts], writes=[c["q"][1]])
            r_ib = 3 * WA + 3 * WB + hb * 128
            cx.dma(cx.sp, c["ib"][0][:], pT[r_ib:r_ib + 128, ts], writes=[c["ib"][1]])
            cx.op(cx.dve, lambda e: e.tensor_scalar(out=c["g"][0][:], in0=c["sig"][0][:], scalar1=oml[0][:, gi:gi + 1],
                                                    scalar2=lb[0][:, gi:gi + 1], op0=ALU.mult, op1=ALU.add),
                  reads=[c["sig"][1], oml[1], lb[1]], writes=[c["g"][1]])
            yield
            cx.op(cx.act, lambda e: e.activation(out=c["g"][0][:], in_=c["g"][0][:], func=AF.Ln),
                  reads=[c["g"][1]], writes=[c["g"][1]])
            yield
            cx.op(cx.dve, lambda e: e.tensor_scalar(out=c["kk"][0][:], in0=c["sig"][0][:], scalar1=noml[0][:, gi:gi + 1],
                                                    scalar2=oml[0][:, gi:gi + 1], op0=ALU.mult, op1=ALU.add),
                  reads=[c["sig"][1], oml[1], noml[1]], writes=[c["kk"][1]])
            yield
            cx.op(cx.dve, lambda e: e.tensor_tensor_scan(out=c["G"][0][:], data0=ones_row[0][:], data1=c["g"][0][:],
                                                         initial=0.0, op0=ALU.mult, op1=ALU.add),
                  reads=[c["g"][1], ones_row[1]], writes=[c["G"][1]])
            yield
            cx.op(cx.dve, lambda e: e.tensor_tensor(out=c["pex"][0][:], in0=c["G"][0][:], in1=c["g"][0][:],
                                                    op=ALU.subtract), reads=[c["G"][1], c["g"][1]], writes=[c["pex"][1]])
            yield
            G3 = c["G"][0][:].rearrange("p (c t) -> p c t", t=64)
            P3 = c["pex"][0][:].rearrange("p (c t) -> p c t", t=64)
            ek3 = c["ek"][0][:].rearrange("p (c t) -> p c t", t=64)
            if d == 0:
                ref = G3[:, :, 63:64].to_broadcast([128, NCH, 64])
                cx.op(cx.dve, lambda e: e.scalar_tensor_tensor(out=ek3, in0=G3, scalar=-1.0, in1=ref,
                                                               op0=ALU.mult, op1=ALU.add),
                      reads=[c["G"][1]], writes=[c["ek"][1]])
                yield
            else:
                ref = P3[:, :, 0:1].to_broadcast([128, NCH, 64])
                cx.op(cx.dve, lambda e: e.tensor_tensor(out=ek3, in0=P3, in1=ref, op=ALU.subtract),
                      reads=[c["pex"][1]], writes=[c["ek"][1]])
                yield
            cx.op(cx.dve, lambda e: e.tensor_tensor(out=c["tot"][0][:], in0=G3[:, :, 63], in1=P3[:, :, 0],
                                                    op=ALU.subtract), reads=[c["G"][1], c["pex"][1]], writes=[c["tot"][1]])
            yield
            cx.op(cx.act, lambda e: e.activation(out=c["dcc"][0][:], in_=c["tot"][0][:], func=AF.Exp),
                  reads=[c["tot"][1]], writes=[c["dcc"][1]])
            yield
            cx.op(cx.act, lambda e: e.activation(out=c["ex"][0][:], in_=c["ek"][0][:], func=AF.Exp, scale=-1.0),
                  reads=[c["ek"][1]], writes=[c["ex"][1]])
            yield
            cx.op(cx.dve, lambda e: e.tensor_tensor(out=c["Qh"][0][:], in0=c["q"][0][:], in1=c["ex"][0][:], op=ALU.mult),
                  reads=[c["q"][1], c["ex"][1]], writes=[c["Qh"][1]])
            yield
            cx.op(cx.act, lambda e: e.activation(out=c["ex"][0][:], in_=c["ek"][0][:], func=AF.Exp),
                  reads=[c["ek"][1]], writes=[c["ex"][1]])
            yield
            cx.op(cx.dve, lambda e: e.tensor_tensor(out=c["Kh"][0][:], in0=c["kk"][0][:], in1=c["ex"][0][:], op=ALU.mult),
                  reads=[c["kk"][1], c["ex"][1]], writes=[c["Kh"][1]])
            yield
            mask = T["mask_f"] if d == 0 else T["mask_b"]
            for kb in range(NK):
                ks = slice(kb * 128, (kb + 1) * 128)
                tp, tpb = tpk.next()
                tpv = tp[:]
                cx.op(cx.pe, lambda e: e.transpose(out=tpv[:, 0:128], in_=c["Kh"][0][:, ks], identity=T["ident"][:]),
                      reads=[c["Kh"][1]], writes=[tpb], inc=False)
                cx.op(cx.pe, lambda e: e.transpose(out=tpv[:, 128:256], in_=c["ib"][0][:, ks], identity=T["ident"][:]),
                      reads=[c["ib"][1]], writes=[tpb])
                if kb % 2 == 0:
                    cx.op(cx.act, lambda e: e.activation(out=c["KV"][0][:, kb, :], in_=tpv[:, 0:256], func=AF.Copy),
                          reads=[tpb], writes=[c["KV"][1]])
                else:
                    cx.op(cx.dve, lambda e: e.tensor_copy(out=c["KV"][0][:, kb, :], in_=tpv[:, 0:256]),
                          reads=[tpb], writes=[c["KV"][1]])
                sc, scb = scp.next()
                cx.op(cx.pe, lambda e: e.matmul(sc[:, 0:128], lhsT=c["Kh"][0][:, ks], rhs=c["Qh"][0][:, ks],
                                                start=True, stop=True), reads=[c["Kh"][1], c["Qh"][1]], writes=[scb])
                cx.op(cx.dve, lambda e: e.tensor_tensor(out=c["AT"][0][:, kb, :], in0=sc[:, 0:128], in1=mask[:],
                                                        op=ALU.mult), reads=[scb], writes=[c["AT"][1]])
                yield

        def seq(c, d, j, first_flags, written, o_acc):
            kbs = range(NK) if d == 0 else range(NK - 1, -1, -1)
            for kb in kbs:
                po, pob = pops[ch.index(c)]
                cx.op(cx.pe, lambda e: e.matmul(po[:, 0:128], lhsT=c["KV"][0][:, kb, 128:256], rhs=c["AT"][0][:, kb, :],
                                                start=True, stop=False), reads=[c["KV"][1], c["AT"][1]], writes=[pob])
                halves = (0, 1) if d == 0 else (1, 0)
                for hi, hf in enumerate(halves):
                    cidx = kb * 2 + hf
                    cols = slice(hf * 64, hf * 64 + 64)
                    toks = slice(kb * 128 + hf * 64, kb * 128 + hf * 64 + 64)
                    if first_flags[d]:
                        first_flags[d] = False
                        cx.op(cx.dve, lambda e: e.memset(c["S"][0][:], 0.0), writes=[c["S"][1]])
                    else:
                        cx.op(cx.dve, lambda e: e.tensor_scalar(out=c["S"][0][:], in0=c["S"][0][:],
                                                                scalar1=c["dcc"][0][:, cidx:cidx + 1], scalar2=None,
                                                                op0=ALU.mult),
                              reads=[c["S"][1], c["dcc"][1]], writes=[c["S"][1]])
                    yield
                    cx.op(cx.act, lambda e: e.activation(out=c["Sbf"][0][:], in_=c["S"][0][:], func=AF.Copy),
                          reads=[c["S"][1]], writes=[c["Sbf"][1]])
                    yield
                    cx.op(cx.pe, lambda e: e.matmul(po[:, cols], lhsT=c["Sbf"][0][:], rhs=c["Qh"][0][:, toks],
                                                    start=False, stop=(hi == 1)),
                          reads=[c["Sbf"][1], c["Qh"][1]], writes=[pob])
                    pst, pstb = pstp.next()
                    cx.op(cx.pe, lambda e: e.matmul(pst[:, 0:128], lhsT=c["KV"][0][hf * 64:hf * 64 + 64, kb, 0:128],
                                                    rhs=c["KV"][0][hf * 64:hf * 64 + 64, kb, 128:256], start=True, stop=True),
                          reads=[c["KV"][1]], writes=[pstb])
                    cx.op(cx.dve, lambda e: e.tensor_tensor(out=c["S"][0][:], in0=c["S"][0][:], in1=pst[:, 0:128],
                                                            op=ALU.add), reads=[c["S"][1], pstb], writes=[c["S"][1]])
                    yield
                og = slice(j * TBH + kb * 128, j * TBH + kb * 128 + 128)
                key = (j, kb)
                if key not in written:
                    written.add(key)
                    cx.op(cx.act, lambda e: e.activation(out=o_acc[0][:, og], in_=po[:, 0:128], func=AF.Copy),
                          reads=[pob], writes=[o_acc[1]])
                else:
                    cx.op(cx.dve, lambda e: e.tensor_tensor(out=o_acc[0][:, og], in0=o_acc[0][:, og], in1=po[:, 0:128],
                                                            op=ALU.add), reads=[pob, o_acc[1]], writes=[o_acc[1]])
                yield

        for hb0 in range(0, HB, HP):
            flags = [[True, True] for _ in range(HP)]
            writtens = [set() for _ in range(HP)]
            for step in range(NBLK):
                js = (step, NBLK - 1 - step)
                pgens = [prep(ch[hp * 2 + d], d, hb0 + hp, js[d]) for hp in range(HP) for d in range(2)]
                palive = [True] * len(pgens)
                while any(palive):
                    for gi in range(len(pgens)):
                        if palive[gi]:
                            try:
                                next(pgens[gi])
                            except StopIteration:
                                palive[gi] = False
                gens = [seq(ch[hp * 2 + d], d, js[d], flags[hp], writtens[hp], o_accs[hp])
                        for hp in range(HP) for d in range(2)]
                alive = [True] * len(gens)
                while any(alive):
                    for gi in range(len(gens)):
                        if alive[gi]:
                            try:
                                next(gens[gi])
                            except StopIteration:
                                alive[gi] = False
            for hp in range(HP):
              hb = hb0 + hp
              o_acc = o_accs[hp]
              r_g = 3 * WA + 4 * WB + hb * 128
              for j in range(NBLK):
                  ts = slice(j * TBH, (j + 1) * TBH)
                  cx.dma(cx.sp, fin_g[0][:], pT[r_g:r_g + 128, ts], writes=[fin_g[1]])
                  cx.op(cx.act, lambda e: e.activation(out=fin_sq[0][:], in_=o_acc[0][:, ts], func=AF.Square),
                        reads=[o_acc[1]], writes=[fin_sq[1]])
                  ps_, psb = scp.next()
                  cx.op(cx.pe, lambda e: e.matmul(ps_[:, :TBH], lhsT=T["ones_f32"][:], rhs=fin_sq[0][:], start=True, stop=True),
                        reads=[fin_sq[1]], writes=[psb])
                  cx.op(cx.act, lambda e: e.activation(out=fin_r[0][:], in_=ps_[:, :TBH], func=AF.Sqrt, bias=T["eps"][:],
                                                       scale=1.0 / 128), reads=[psb], writes=[fin_r[1]])
                  cx.op(cx.dve, lambda e: e.reciprocal(out=fin_r[0][:], in_=fin_r[0][:]), reads=[fin_r[1]], writes=[fin_r[1]])
                  cx.op(cx.dve, lambda e: e.scalar_tensor_tensor(out=fin_t[0][:], in0=o_acc[0][:, ts],
                                                                 scalar=P["hnw"][:, l:l + 1], in1=fin_r[0][:],
                                                                 op0=ALU.mult, op1=ALU.mult),
                        reads=[o_acc[1], fin_r[1]], writes=[fin_t[1]])
                  fo, fob = fin_o.next()
                  cx.op(cx.dve, lambda e: e.tensor_tensor(out=fo[:], in0=fin_t[0][:], in1=fin_g[0][:], op=ALU.mult),
                        reads=[fin_t[1], fin_g[1]], writes=[fob])
                  cx.dma(cx.sp, bT[hb * 128:hb * 128 + 128, ts], fo[:], reads=[fob])


def branch_phase(cx, cfg, l, T, which):
    D, S, WA, WB = cfg.D, cfg.S, cfg.WA, cfg.WB
    pT = T["pT"].ap()
    mA = T["mA"].ap()
    mT = T["mT"].ap()
    g0 = 3 * WA + 5 * WB + (0 if which == 0 else D)
    with Scope(cx):
        gt = Rot([(cx.sbuf([128, 512], BF16), Buf()) for _ in range(6)])
        at = Rot([(cx.sbuf([128, 512], F32), Buf()) for _ in range(6)])
        of = Rot([(cx.sbuf([128, 512], F32), Buf()) for _ in range(6)])
        ob = Rot([(cx.sbuf([128, 512], BF16), Buf()) for _ in range(6)])

        def epi(pt, pb, n0, t0, w):
            g, gb = gt.next()
            cx.dma(cx.act, g[:, :w], pT[g0 + n0:g0 + n0 + 128, t0:t0 + w], writes=[gb])
            if which == 0:
                o, obuf = of.next()
                cx.op(cx.dve, lambda e: e.tensor_tensor(out=o[:, :w], in0=pt[:, :w], in1=g[:, :w], op=ALU.mult),
                      reads=[pb, gb], writes=[obuf])
                cx.dma(cx.sp, mA[n0:n0 + 128, t0:t0 + w], o[:, :w], reads=[obuf])
            else:
                a, ab = at.next()
                cx.dma(cx.act, a[:, :w], mA[n0:n0 + 128, t0:t0 + w], writes=[ab])
                o, obuf = of.next()
                cx.op(cx.dve, lambda e: e.tensor_tensor(out=o[:, :w], in0=pt[:, :w], in1=g[:, :w], op=ALU.mult),
                      reads=[pb, gb], writes=[obuf])
                o2, o2b = ob.next()
                cx.op(cx.dve, lambda e: e.tensor_tensor(out=o2[:, :w], in0=o[:, :w], in1=a[:, :w], op=ALU.add),
                      reads=[obuf, ab], writes=[o2b])
                cx.dma(cx.sp, mT[n0:n0 + 128, t0:t0 + w], o2[:, :w], reads=[o2b])

        if which == 0:
            gemm_phase(cx, cfg, T["w_branch_a"].ap()[l], T["aT"].ap(), Buf(), WA, D, S, epi)
        else:
            gemm_phase(cx, cfg, T["w_branch_b"].ap()[l], T["bT"].ap(), Buf(), WB, D, S, epi)


def resid_gemm_phase(cx, cfg, W, xin, K, T):
    D, S = cfg.D, cfg.S
    xT = T["xT"].ap()
    with Scope(cx):
        depth = 6 if K <= 4096 else 3
        xt = Rot([(cx.sbuf([128, 512], F32), Buf()) for _ in range(depth)])
        ot = Rot([(cx.sbuf([128, 512], F32), Buf()) for _ in range(depth)])

        def epi(pt, pb, n0, t0, w):
            x_, xb = xt.next()
            cx.dma(cx.act, x_[:, :w], xT[n0:n0 + 128, t0:t0 + w], writes=[xb])
            o, obuf = ot.next()
            cx.op(cx.dve, lambda e: e.tensor_tensor(out=o[:, :w], in0=pt[:, :w], in1=x_[:, :w], op=ALU.add),
                  reads=[pb, xb], writes=[obuf])
            cx.dma(cx.sp, xT[n0:n0 + 128, t0:t0 + w], o[:, :w], reads=[obuf])

        gemm_phase(cx, cfg, W, xin, Buf(), K, D, S, epi)


def up_phase(cx, cfg, l, T):
    D, S, DFF = cfg.D, cfg.S, cfg.DFF
    uT = T["uT"].ap()
    with Scope(cx):
        st = Rot([(cx.sbuf([128, 512], BF16), Buf()) for _ in range(4)])
        evac = make_evac(cx)

        def epi(pt, pb, n0, t0, w):
            s_, sb_ = st.next()
            evac(s_[:, :w], pt[:, :w], [pb], [sb_])
            cx.dma(cx.sp, uT[n0:n0 + 128, t0:t0 + w], s_[:, :w], reads=[sb_])

        gemm_phase(cx, cfg, T["w_up"].ap()[l], T["hT"].ap(), Buf(), D, 2 * DFF, S, epi)


def conv_phase(cx, cfg, l, T, P):
    S, DFF = cfg.S, cfg.DFF
    NFC = 2 * DFF // 128
    NG = DFF // 128
    TW = min(512, S)
    uT = T["uT"].ap()
    actT = T["actT"].ap()
    cw, cb = P["convw"], P["convb"]
    with Scope(cx):
        ug = [(cx.sbuf([128, S + 2], BF16), Buf()) for _ in range(2)]
        uv = [(cx.sbuf([128, S + 2], BF16), Buf()) for _ in range(2)]
        dg = [(cx.sbuf([128, 6, 128], BF16), Buf()) for _ in range(2)]
        sg = Rot([(cx.sbuf([128, TW], F32), Buf()) for _ in range(3)])
        oo = [(cx.sbuf([128, S], BF16), Buf()) for _ in range(2)]
        pg = Rot([(cx.psum([128, 512], F32), Buf()) for _ in range(3)])
        pv = Rot([(cx.psum([128, 512], F32), Buf()) for _ in range(3)])
        for t_, b_ in ug + uv:
            cx.op(cx.dve, lambda e: e.memset(t_[:, 0:1], 0.0), writes=[b_])
            cx.op(cx.dve, lambda e: e.memset(t_[:, S + 1:S + 2], 0.0), writes=[b_])
        for c in range(NG):
            k = c % 2
            cx.dma(cx.sp, ug[k][0][:, 1:S + 1], uT[c * 128:(c + 1) * 128, :], writes=[ug[k][1]])
            cx.dma(cx.sp, uv[k][0][:, 1:S + 1], uT[DFF + c * 128:DFF + (c + 1) * 128, :], writes=[uv[k][1]])
            for j in range(3):
                for gi, cc in enumerate((c, NG + c)):
                    col = (l * 3 + j) * NFC + cc
                    cx.op(cx.dve, lambda e: e.tensor_scalar(out=dg[k][0][:, gi * 3 + j, :], in0=T["ident"],
                                                            scalar1=cw[:, col:col + 1], scalar2=None, op0=ALU.mult),
                          writes=[dg[k][1]])
            for tb in range(S // TW):
                a, ab = pg.next()
                v, vb = pv.next()
                for j in range(3):
                    cx.op(cx.pe, lambda e: e.matmul(a[:, :TW], lhsT=dg[k][0][:, j, :],
                                                    rhs=ug[k][0][:, tb * TW + j:tb * TW + j + TW],
                                                    start=(j == 0), stop=(j == 2)),
                          reads=[dg[k][1], ug[k][1]], writes=[ab], inc=(j == 2))
                for j in range(3):
                    cx.op(cx.pe, lambda e: e.matmul(v[:, :TW], lhsT=dg[k][0][:, 3 + j, :],
                                                    rhs=uv[k][0][:, tb * TW + j:tb * TW + j + TW],
                                                    start=(j == 0), stop=(j == 2)),
                          reads=[dg[k][1], uv[k][1]], writes=[vb], inc=(j == 2))
                s_, sb_ = sg.next()
                cx.op(cx.act, lambda e: e.activation(out=s_[:], in_=a[:, :TW], func=AF.Silu,
                                                     bias=cb[:, l * NFC + c:l * NFC + c + 1]),
                      reads=[ab], writes=[sb_])
                cx.op(cx.dve, lambda e: e.scalar_tensor_tensor(out=oo[k][0][:, tb * TW:(tb + 1) * TW], in0=v[:, :TW],
                                                               scalar=cb[:, l * NFC + NG + c:l * NFC + NG + c + 1],
                                                               in1=s_[:], op0=ALU.add, op1=ALU.mult),
                      reads=[vb, sb_], writes=[oo[k][1]])
            cx.dma(cx.sp, actT[c * 128:(c + 1) * 128, :], oo[k][0][:], reads=[oo[k][1]])


def param_layout(cfg):
    DC = cfg.D // 128
    NFC = 2 * cfg.DFF // 128
    segs = [("n1", cfg.DEPTH * DC), ("n2", cfg.DEPTH * DC), ("nf", DC), ("lam", cfg.DEPTH * 512),
            ("subln", cfg.DEPTH * 2), ("lbl", 2 * cfg.HB * cfg.DEPTH), ("hnw", cfg.DEPTH),
            ("convw", cfg.DEPTH * 3 * NFC), ("convb", cfg.DEPTH * NFC), ("mask_f", 128), ("mask_b", 128)]
    off, o = {}, 0
    for n, w in segs:
        off[n] = (o, o + w)
        o += w
    return off, o


def build(cfg, nlayers=None, stop_after=None):
    D, S, DEPTH = cfg.D, cfg.S, cfg.DEPTH
    nlayers = DEPTH if nlayers is None else nlayers
    nc = bass.Bass("TRN2", target_bir_lowering=False)
    T = {}

    def din(name, shape, dt=F32):
        T[name] = nc.dram_tensor(name, list(shape), dt, kind="ExternalInput")

    din("x_in", [D, S])
    din("w_in", [DEPTH, D, cfg.N_IN])
    din("w_branch_a", [DEPTH, cfg.WA, D])
    din("w_branch_b", [DEPTH, cfg.WB, D])
    din("w_out", [DEPTH, D, D])
    din("w_up", [DEPTH, D, 2 * cfg.DFF])
    din("w_down", [DEPTH, cfg.DFF, D])
    poff, NP = param_layout(cfg)
    din("params", [128, NP])
    din("c_cos", [128, S])
    din("c_sin", [128, S])
    din("c_bf", [128, 256], BF16)
    T["out"] = nc.dram_tensor("out", [D, S], F32, kind="ExternalOutput")
    for name, shape, dt in (("xT", [D, S], F32), ("hT", [D, S], BF16), ("pT", [cfg.N_IN, S], BF16),
                            ("sgT", [2 * cfg.WB, S], F32), ("aT", [cfg.WA, S], BF16), ("bT", [cfg.WB, S], BF16),
                            ("mA", [D, S], F32), ("mT", [D, S], BF16), ("uT", [2 * cfg.DFF, S], BF16),
                            ("actT", [cfg.DFF, S], BF16)):
        T[name] = nc.dram_tensor(name, shape, dt)
    cx = Ctx(nc)
    with cx.stack:
        par = cx.sbuf([128, NP], F32, "par")
        cbf = cx.sbuf([128, 256], BF16, "cbf")
        T["ones_f32"] = cx.sbuf([128, 128], F32, "ones_f32")
        T["ones_bf"] = cx.sbuf([128, 128], BF16, "ones_bf")
        T["eps"] = cx.sbuf([128, 1], F32, "eps")
        cb = Buf()
        cx.dma(cx.sp, par[:], T["params"].ap(), writes=[cb])
        cx.dma(cx.sp, cbf[:], T["c_bf"].ap(), writes=[cb])
        cx.op(cx.dve, lambda e: e.memset(T["ones_f32"][:], 1.0), writes=[Buf()])
        cx.op(cx.dve, lambda e: e.memset(T["ones_bf"][:], 1.0), writes=[Buf()])
        cx.op(cx.dve, lambda e: e.memset(T["eps"][:], cfg.EPS), writes=[Buf()])
        P = {n: par[:, a:b] for n, (a, b) in poff.items()}
        T["ident"] = cbf[:, 0:128]
        T["rot"] = cbf[:, 128:256]
        T["mask_f"] = P["mask_f"]
        T["mask_b"] = P["mask_b"]
        xi = T["x_in"].ap()
        xo = T["xT"].ap()
        nparts = 8
        rows = D // nparts
        for i in range(nparts):
            cx.dma(cx.sp, xo[i * rows:(i + 1) * rows, :], xi[i * rows:(i + 1) * rows, :])
        cx.barrier()
        DC = D // 128
        done = False
        for l in range(nlayers):
            rmsnorm_phase(cx, cfg, T["xT"].ap(), Buf(), P["n1"][:, l * DC:(l + 1) * DC], T["hT"].ap(), Buf(), T)
            proj_phase(cx, cfg, l, T)
            if stop_after == "proj":
                break
            attn_phase(cx, cfg, l, T, P)
            if stop_after == "attn":
                break
            hgrn_phase(cx, cfg, l, T, P)
            if stop_after == "hgrn":
                break
            branch_phase(cx, cfg, l, T, 0)
            branch_phase(cx, cfg, l, T, 1)
            resid_gemm_phase(cx, cfg, T["w_out"].ap()[l], T["mT"].ap(), D, T)
            if stop_after == "mixer":
                break
            rmsnorm_phase(cx, cfg, T["xT"].ap(), Buf(), P["n2"][:, l * DC:(l + 1) * DC], T["hT"].ap(), Buf(), T)
            up_phase(cx, cfg, l, T)
            conv_phase(cx, cfg, l, T, P)
            resid_gemm_phase(cx, cfg, T["w_down"].ap()[l], T["actT"].ap(), cfg.DFF, T)
        rmsnorm_phase(cx, cfg, T["xT"].ap(), Buf(), P["nf"], T["out"].ap(), Buf(), T, out_f32=True)
        cx.finish([])
    return nc, T


def host_consts(cfg):
    import ml_dtypes
    S = cfg.S
    inv = 1.0 / (10000.0 ** (np.arange(0, 128, 2, dtype=np.float32) / 128))
    ang = np.arange(S, dtype=np.float32)[:, None] * inv[None, :]
    cos = np.cos(ang).astype(np.float32).T
    sin = np.sin(ang).astype(np.float32).T
    c_cos = np.ascontiguousarray(np.concatenate([cos, cos], 0))
    c_sin = np.ascontiguousarray(np.concatenate([sin, sin], 0))
    ident = np.eye(128, dtype=np.float32)
    rot = np.zeros((128, 128), np.float32)
    for m in range(64):
        rot[m + 64, m] = -1.0
        rot[m, m + 64] = 1.0
    c_bf = np.concatenate([ident, rot], 1).astype(ml_dtypes.bfloat16)
    s_ = np.arange(128)[:, None]
    t_ = np.arange(128)[None, :]
    same = (s_ // 64) == (t_ // 64)
    mask_f = (same & (s_ <= t_)).astype(np.float32)
    mask_b = (same & (s_ >= t_)).astype(np.float32)
    return c_cos, c_sin, c_bf, mask_f, mask_b


def host_params(cfg, inp):
    D, DEPTH, HB = cfg.D, cfg.DEPTH, cfg.HB
    DC = D // 128
    NFC = 2 * cfg.DFF // 128
    poff, NP = param_layout(cfg)
    par = np.zeros((128, NP), np.float32)

    def put(name, arr):
        a, b = poff[name]
        par[:, a:b] = arr.reshape(128, b - a)

    f = lambda k: np.asarray(inp[k], np.float32)
    put("n1", f("norm1_w").reshape(DEPTH, DC, 128).transpose(2, 0, 1))
    put("n2", f("norm2_w").reshape(DEPTH, DC, 128).transpose(2, 0, 1))
    put("nf", f("final_norm_w").reshape(DC, 128).transpose(1, 0))
    put("lam", np.broadcast_to(f("diff_lambda").reshape(1, DEPTH * 512), (128, DEPTH * 512)))
    put("subln", f("diff_subln_w").reshape(DEPTH, 2, 128).transpose(2, 0, 1))
    put("lbl", f("hgrn_lb_logits").reshape(DEPTH, 2, HB, 128).transpose(3, 1, 2, 0))
    put("hnw", f("hgrn_norm_w").reshape(DEPTH, 128).transpose(1, 0))
    put("convw", f("conv_w").reshape(DEPTH, 3, NFC, 128).transpose(3, 0, 1, 2))
    put("convb", f("conv_b").reshape(DEPTH, NFC, 128).transpose(2, 0, 1))
    _, _, _, mask_f, mask_b = host_consts(cfg)
    put("mask_f", mask_f)
    put("mask_b", mask_b)
    return par


_CACHE = {}


def kernel(**inputs):
    x = np.asarray(inputs["x"], np.float32)
    B, S, D = x.shape
    DEPTH = np.asarray(inputs["w_in"]).shape[0]
    cfg = Cfg(D=D, S=S, DEPTH=DEPTH)
    key = (D, S, DEPTH)
    if key not in _CACHE:
        _CACHE[key] = build(cfg)
    nc, T = _CACHE[key]
    c_cos, c_sin, c_bf, _, _ = host_consts(cfg)
    par = host_params(cfg, inputs)
    ncores = B
    shared = {k: np.ascontiguousarray(np.asarray(inputs[k], np.float32)) for k in
              ("w_in", "w_branch_a", "w_branch_b", "w_out", "w_up", "w_down")}
    in_maps = []
    for c in range(ncores):
        b = c % B
        m = dict(shared)
        m["x_in"] = np.ascontiguousarray(x[b].T)
        m["params"] = par
        m["c_cos"] = c_cos
        m["c_sin"] = c_sin
        m["c_bf"] = c_bf
        in_maps.append(m)
    res = run_bass_kernel_spmd(nc, in_maps, core_ids=list(range(ncores)))
    out = np.stack([np.ascontiguousarray(res.results[b]["out"].T) for b in range(B)], 0)
    return out.astype(np.float32)
```

```python
from contextlib import ExitStack
import numpy as np
import concourse.bass as bass
import concourse.mybir as mybir
from concourse.bass_utils import run_bass_kernel_spmd

F32 = mybir.dt.float32
BF16 = mybir.dt.bfloat16
AF = mybir.ActivationFunctionType
ALU = mybir.AluOpType


class Buf:
    __slots__ = ("name", "w", "r")

    def __init__(self, name=""):
        self.name = name
        self.w = {}
        self.r = {}


class Q:
    def __init__(self, cx, eng, name, n_dma_sems=0):
        self.cx = cx
        self.e = eng
        self.name = name
        self.sem = cx.new_sem("q_" + name)
        self.count = 0
        self.seen = {}
        self.dsems = [[cx.new_sem(f"d_{name}{i}"), 0] for i in range(n_dma_sems)]
        self.dnext = 0


class Ctx:
    def __init__(self, nc):
        self.nc = nc
        self.stack = ExitStack()
        self.nsem = 0
        self.pe = Q(self, nc.tensor, "pe")
        self.act = Q(self, nc.scalar, "act", 12)
        self.dve = Q(self, nc.vector, "dve")
        self.pool = Q(self, nc.gpsimd, "pool", 24)
        self.sp = Q(self, nc.sync, "sp", 40)
        self.uid = 0

    def new_sem(self, name):
        self.nsem += 1
        return self.stack.enter_context(self.nc.semaphore(name))

    def sbuf(self, shape, dtype, name=None):
        self.uid += 1
        return self.stack.enter_context(
            self.nc.sbuf_tensor(name or f"sb{self.uid}", list(shape), dtype))

    def psum(self, shape, dtype, name=None):
        self.uid += 1
        return self.stack.enter_context(
            self.nc.psum_tensor(name or f"ps{self.uid}", list(shape), dtype))

    def _waits(self, q, reads, writes, extra=None):
        need = {}
        for b in reads:
            for s, v in b.w.items():
                if need.get(s, 0) < v:
                    need[s] = v
        for b in writes:
            for d in (b.w, b.r):
                for s, v in d.items():
                    if need.get(s, 0) < v:
                        need[s] = v
        if extra:
            for s, v in extra:
                if need.get(s, 0) < v:
                    need[s] = v
        for s, v in need.items():
            if s is q.sem and q is self.pe:
                continue
            if q.seen.get(s, 0) >= v:
                continue
            q.e.wait_ge(s, v)
            q.seen[s] = v

    def op(self, q, fn, reads=(), writes=(), inc=True):
        self._waits(q, reads, writes)
        ins = fn(q.e)
        val = q.count + 1
        if inc:
            ins.then_inc(q.sem, 1)
            q.count = val
        for b in writes:
            b.w = {q.sem: val}
            b.r = {}
        for b in reads:
            if b.r.get(q.sem, 0) < val:
                b.r[q.sem] = val
        return ins

    def dma(self, q, out, in_, reads=(), writes=(), **kw):
        slot = q.dsems[q.dnext]
        q.dnext = (q.dnext + 1) % len(q.dsems)
        sem, cnt = slot
        self._waits(q, reads, writes, extra=[(sem, cnt)] if cnt else None)
        ins = q.e.dma_start(out=out, in_=in_, **kw)
        cnt += 16
        slot[1] = cnt
        ins.then_inc(sem, 16)
        for b in writes:
            b.w = {sem: cnt}
            b.r = {}
        for b in reads:
            b.r[sem] = cnt
        return ins

    def barrier(self):
        need = {}
        qs = (self.pe, self.act, self.dve, self.pool, self.sp)
        for qq in qs:
            if qq.count:
                need[qq.sem] = qq.count
            for s, c in qq.dsems:
                if c:
                    need[s] = c
        for q in qs:
            for s, v in need.items():
                if s is q.sem:
                    continue
                if q.seen.get(s, 0) < v:
                    q.e.wait_ge(s, v)
                    q.seen[s] = v

    def finish(self, bufs):
        q = self.sp
        need = {}
        for qq in (self.pe, self.act, self.dve, self.pool, self.sp):
            if qq.count:
                need[qq.sem] = qq.count
            for s, c in qq.dsems:
                if c:
                    need[s] = c
        for s, v in need.items():
            if q.seen.get(s, 0) < v:
                q.e.wait_ge(s, v)
                q.seen[s] = v


class Rot:
    def __init__(self, items):
        self.items = items
        self.i = 0

    def next(self):
        it = self.items[self.i]
        self.i = (self.i + 1) % len(self.items)
        return it


class Cfg:
    def __init__(self, D=4096, S=4096, DEPTH=4):
        self.D = D
        self.S = S
        self.DEPTH = DEPTH
        self.WA = D // 2
        self.HDA = 128
        self.HA = self.WA // 256
        self.WB = D // 2
        self.HB = self.WB // 128
        self.DFF = 7 * D // 4
        self.N_IN = 3 * self.WA + 5 * self.WB + 2 * D
        self.EPS = 1e-6
        self.CHUNK = 64


def rmsnorm_phase(cx, cfg, x_dram, xbuf, w_sb, h_dram, hbuf, consts, out_f32=False):
    nc = cx.nc
    D, S = cfg.D, cfg.S
    DC = D // 128
    TBK = min(256, S)
    xv = x_dram.rearrange("(c p) s -> p c s", p=128)
    hv = h_dram.rearrange("(c p) s -> p c s", p=128)
    with ExitStack() as st:
        old = cx.stack
        cx.stack = st
        xt = Rot([(cx.sbuf([128, DC, TBK], F32), Buf()) for _ in range(2)])
        ht = Rot([(cx.sbuf([128, DC, TBK], F32 if out_f32 else BF16), Buf()) for _ in range(2)])
        sq = Rot([(cx.sbuf([128, TBK], F32), Buf()) for _ in range(3)])
        rstd = (cx.sbuf([128, TBK], F32), Buf())
        ps = (cx.psum([128, TBK], F32), Buf())
        NB_ = S // TBK
        nxt_x = xt.next()
        cx.dma(cx.sp, nxt_x[0][:], xv[:, :, 0:TBK], reads=[xbuf], writes=[nxt_x[1]])
        for tb in range(NB_):
            xs, xb = nxt_x
            hs, hb = ht.next()
            sl = slice(tb * TBK, (tb + 1) * TBK)
            if tb + 1 < NB_:
                nxt_x = xt.next()
                cx.dma(cx.sp, nxt_x[0][:], xv[:, :, (tb + 1) * TBK:(tb + 2) * TBK], reads=[xbuf], writes=[nxt_x[1]])
            for c in range(DC):
                sqs, sqb = sq.next()
                cx.op(cx.act, lambda e: e.activation(out=sqs[:], in_=xs[:, c, :], func=AF.Square),
                      reads=[xb], writes=[sqb])
                cx.op(cx.pe, lambda e: e.matmul(ps[0][:], lhsT=consts["ones_f32"][:], rhs=sqs[:],
                                                start=(c == 0), stop=(c == DC - 1)),
                      reads=[sqb], writes=[ps[1]])
            cx.op(cx.act, lambda e: e.activation(out=rstd[0][:], in_=ps[0][:], func=AF.Sqrt,
                                                 bias=consts["eps"][:], scale=1.0 / D),
                  reads=[ps[1]], writes=[rstd[1]])
            cx.op(cx.dve, lambda e: e.reciprocal(out=rstd[0][:], in_=rstd[0][:]),
                  reads=[rstd[1]], writes=[rstd[1]])
            for c in range(DC):
                cx.op(cx.dve, lambda e: e.scalar_tensor_tensor(
                    out=hs[:, c, :], in0=xs[:, c, :], scalar=w_sb[:, c:c + 1], in1=rstd[0][:],
                    op0=ALU.mult, op1=ALU.mult),
                    reads=[xb, rstd[1]], writes=[hb])
            cx.dma(cx.sp, hv[:, :, sl], hs[:], reads=[hb], writes=[hbuf])
        cx.barrier()
        cx.stack = old


def gemm_phase(cx, cfg, W, xin, xin_buf, K, N, S, epilogue, TB=None, CW=512):
    KC = K // 128
    if TB is None:
        TB = 1024 if KC <= 32 else 512
    TB = min(TB, S)
    CW = min(CW, N)
    while N % CW:
        CW //= 2
    SBK = min(512, TB)
    Wv = W.rearrange("(kc p) n -> p kc n", p=128)
    Xv = xin.rearrange("(kc p) s -> p kc s", p=128)
    with ExitStack() as st:
        old = cx.stack
        cx.stack = st
        NSB = TB // SBK
        NTB = S // TB
        xh = [(cx.sbuf([128, KC, SBK], BF16), Buf()) for _ in range(NSB)]
        wsl = Rot([(cx.sbuf([128, KC, CW], BF16), Buf()) for _ in range(2)])
        banks = Rot([(cx.psum([128, 512], F32), Buf()) for _ in range(4)])
        NG = N // CW

        def load_x(tb, sb):
            t0 = tb * TB + sb * SBK
            cx.dma(cx.sp, xh[sb][0][:], Xv[:, :, t0:t0 + SBK], reads=[xin_buf], writes=[xh[sb][1]])

        for sb in range(NSB):
            load_x(0, sb)
        cur = wsl.next()
        cx.dma(cx.pool, cur[0][:], Wv[:, :, 0:CW], writes=[cur[1]])
        for tb in range(NTB):
            for ng in range(NG):
                nxt = None
                if ng + 1 < NG or tb + 1 < NTB:
                    g2 = (ng + 1) % NG
                    nxt = wsl.next()
                    cx.dma(cx.pool, nxt[0][:], Wv[:, :, g2 * CW:(g2 + 1) * CW], writes=[nxt[1]])
                ws, wb = cur
                for sb in range(NSB):
                    for nt in range(CW // 128):
                        pt, pb = banks.next()
                        for kc in range(KC):
                            cx.op(cx.pe, lambda e: e.matmul(
                                pt[:, :SBK], lhsT=ws[:, kc, nt * 128:(nt + 1) * 128],
                                rhs=xh[sb][0][:, kc, :],
                                start=(kc == 0), stop=(kc == KC - 1)),
                                reads=[wb, xh[sb][1]], writes=[pb], inc=(kc == KC - 1))
                        epilogue(pt, pb, ng * CW + nt * 128, tb * TB + sb * SBK, SBK)
                    if ng == NG - 1 and tb + 1 < NTB:
                        load_x(tb + 1, sb)
                cur = nxt
        cx.barrier()
        cx.stack = old


class Scope:
    def __init__(self, cx):
        self.cx = cx

    def __enter__(self):
        self.st = ExitStack()
        self.st.__enter__()
        self.old = self.cx.stack
        self.cx.stack = self.st
        return self

    def __exit__(self, *a):
        self.cx.barrier()
        self.cx.stack = self.old
        return self.st.__exit__(*a)


def make_evac(cx):
    flip = [0]

    def evac(out, in_, reads, writes, func=None, scale=1.0):
        flip[0] ^= 1
        if func is not None or flip[0]:
            f = func if func is not None else AF.Copy
            cx.op(cx.act, lambda e: e.activation(out=out, in_=in_, func=f, scale=scale),
                  reads=reads, writes=writes)
        else:
            cx.op(cx.dve, lambda e: e.tensor_copy(out=out, in_=in_), reads=reads, writes=writes)
    return evac


def proj_phase(cx, cfg, l, T):
    D, S, WA, WB = cfg.D, cfg.S, cfg.WA, cfg.WB
    segs = []
    o = 0
    for name, w in (("qa", WA), ("ka", WA), ("va", WA), ("qb", WB), ("ff", WB), ("fb", WB),
                    ("ib", WB), ("gb", WB), ("ga", D), ("gg", D)):
        segs.append((o, o + w, name))
        o += w
    pT = T["pT"].ap()
    sgT = T["sgT"].ap()
    cosv, sinv = T["c_cos"].ap(), T["c_sin"].ap()
    with Scope(cx):
        st_bf = Rot([(cx.sbuf([128, 512], BF16), Buf()) for _ in range(4)])
        st_f = Rot([(cx.sbuf([128, 512], F32), Buf()) for _ in range(3)])
        xr = Rot([(cx.sbuf([128, 512], BF16), Buf()) for _ in range(2)])
        t1 = Rot([(cx.sbuf([128, 512], F32), Buf()) for _ in range(2)])
        t2 = Rot([(cx.sbuf([128, 512], F32), Buf()) for _ in range(2)])
        CSW = min(1024, S)
        cs = [(cx.sbuf([128, CSW], F32), Buf()), (cx.sbuf([128, CSW], F32), Buf())]
        rps = (cx.psum([128, 512], F32), Buf())
        cs_t0 = [-1]

        def epi(pt, pb, n0, t0, w):
            kind = [k for (a, b, k) in segs if a <= n0 < b][0]
            import os as _os
            dbg = _os.environ.get("PROJ_DBG", "")
            if dbg == "simple":
                kind = "va"
            elif dbg == "norope" and kind in ("qa", "ka"):
                kind = "va"
            elif dbg == "nosg" and kind in ("ff", "fb"):
                kind = "va"
            if kind in ("qa", "ka"):
                cblk = t0 // CSW
                co = t0 % CSW
                if cs_t0[0] != cblk:
                    cs_t0[0] = cblk
                    cx.dma(cx.sp, cs[0][0][:], cosv[:, cblk * CSW:(cblk + 1) * CSW], writes=[cs[0][1]])
                    cx.dma(cx.sp, cs[1][0][:], sinv[:, cblk * CSW:(cblk + 1) * CSW], writes=[cs[1][1]])
                xs, xb = xr.next()
                cx.op(cx.act, lambda e: e.activation(out=xs[:, :w], in_=pt[:, :w], func=AF.Copy),
                      reads=[pb], writes=[xb])
                cx.op(cx.pe, lambda e: e.matmul(rps[0][:, :w], lhsT=T["rot"][:], rhs=xs[:, :w],
                                                start=True, stop=True), reads=[xb], writes=[rps[1]])
                a, ab = t1.next()
                cx.op(cx.dve, lambda e: e.tensor_tensor(out=a[:, :w], in0=xs[:, :w], in1=cs[0][0][:, co:co + w],
                                                        op=ALU.mult), reads=[xb, cs[0][1]], writes=[ab])
                b_, bb = t2.next()
                cx.op(cx.dve, lambda e: e.tensor_tensor(out=b_[:, :w], in0=rps[0][:, :w], in1=cs[1][0][:, co:co + w],
                                                        op=ALU.mult), reads=[rps[1], cs[1][1]], writes=[bb])
                ss, sb_ = st_bf.next()
                cx.op(cx.dve, lambda e: e.tensor_tensor(out=ss[:, :w], in0=a[:, :w], in1=b_[:, :w], op=ALU.add),
                      reads=[ab, bb], writes=[sb_])
                cx.dma(cx.sp, pT[n0:n0 + 128, t0:t0 + w], ss[:, :w], reads=[sb_])
            elif kind in ("ff", "fb"):
                ss, sb_ = st_f.next()
                cx.op(cx.act, lambda e: e.activation(out=ss[:, :w], in_=pt[:, :w], func=AF.Sigmoid),
                      reads=[pb], writes=[sb_])
                r0 = n0 - segs[4][0]
                cx.dma(cx.sp, sgT[r0:r0 + 128, t0:t0 + w], ss[:, :w], reads=[sb_])
            else:
                f = {"va": AF.Copy, "ib": AF.Copy, "qb": AF.Silu, "gb": AF.Silu,
                     "ga": AF.Sigmoid, "gg": AF.Sigmoid}[kind]
                ss, sb_ = st_bf.next()
                if f == AF.Copy and (n0 // 128) % 2:
                    cx.op(cx.dve, lambda e: e.tensor_copy(out=ss[:, :w], in_=pt[:, :w]), reads=[pb], writes=[sb_])
                else:
                    cx.op(cx.act, lambda e: e.activation(out=ss[:, :w], in_=pt[:, :w], func=f),
                          reads=[pb], writes=[sb_])
                cx.dma(cx.sp, pT[n0:n0 + 128, t0:t0 + w], ss[:, :w], reads=[sb_])

        gemm_phase(cx, cfg, T["w_in"].ap()[l], T["hT"].ap(), Buf(), D, cfg.N_IN, S, epi)


def attn_phase(cx, cfg, l, T, P):
    S, WA, HA = cfg.S, cfg.WA, cfg.HA
    KB = S // 128
    QBW = min(512, S)
    NQB = S // QBW
    pT = T["pT"].ap()
    aT = T["aT"].ap()
    scale = 128 ** -0.5
    lam_init = 0.8 - 0.6 * float(np.exp(-0.3 * l))
    with Scope(cx):
        lp = P["lam"]
        lt = (cx.sbuf([128, 256], F32), Buf())
        ls = (cx.sbuf([128, 4], F32), Buf())
        base = l * 512
        cx.op(cx.dve, lambda e: e.tensor_tensor(out=lt[0][:, 0:128], in0=lp[:, base:base + 128],
                                                in1=lp[:, base + 128:base + 256], op=ALU.mult), writes=[lt[1]])
        cx.op(cx.dve, lambda e: e.tensor_tensor(out=lt[0][:, 128:256], in0=lp[:, base + 256:base + 384],
                                                in1=lp[:, base + 384:base + 512], op=ALU.mult), writes=[lt[1]])
        cx.op(cx.dve, lambda e: e.reduce_sum(out=ls[0][:, 0:1], in_=lt[0][:, 0:128], axis=mybir.AxisListType.X),
              reads=[lt[1]], writes=[ls[1]])
        cx.op(cx.dve, lambda e: e.reduce_sum(out=ls[0][:, 1:2], in_=lt[0][:, 128:256], axis=mybir.AxisListType.X),
              reads=[lt[1]], writes=[ls[1]])
        cx.op(cx.act, lambda e: e.activation(out=ls[0][:, 0:2], in_=ls[0][:, 0:2], func=AF.Exp),
              reads=[ls[1]], writes=[ls[1]])
        cx.op(cx.dve, lambda e: e.tensor_tensor(out=ls[0][:, 2:3], in0=ls[0][:, 1:2], in1=ls[0][:, 0:1],
                                                op=ALU.subtract), reads=[ls[1]], writes=[ls[1]])
        cx.op(cx.dve, lambda e: e.tensor_scalar(out=ls[0][:, 3:4], in0=ls[0][:, 2:3], scalar1=-lam_init,
                                                scalar2=None, op0=ALU.add), reads=[ls[1]], writes=[ls[1]])
        neg_lam = ls[0][:, 3:4]
        sw = (cx.sbuf([128, 2], F32), Buf())
        cx.op(cx.dve, lambda e: e.tensor_scalar(out=sw[0][:], in0=P["subln"][:, 2 * l:2 * l + 2],
                                                scalar1=1.0 - lam_init, scalar2=None, op0=ALU.mult), writes=[sw[1]])
        vT = (cx.sbuf([128, 2, S], BF16), Buf())
        vtok = (cx.sbuf([128, KB, 256], BF16), Buf())
        kq = [[(cx.sbuf([128, S], BF16), Buf()) for _ in range(2)] for _ in range(2)]
        E = [(cx.sbuf([128, KB, QBW], BF16), Buf()) for _ in range(2)]
        sps = Rot([(cx.psum([128, 512], F32), Buf()) for _ in range(2)])
        pos = [[(cx.psum([128, 512], F32), Buf()) for _ in range(3)] for _ in range(2)]
        tps = sps
        rz = [(cx.sbuf([128, QBW], F32), Buf()) for _ in range(2)]
        ta = Rot([(cx.sbuf([128, QBW], F32), Buf()) for _ in range(2)])
        tb_ = Rot([(cx.sbuf([128, QBW], F32), Buf()) for _ in range(2)])
        od = [(cx.sbuf([128, QBW], F32), Buf()) for _ in range(2)]
        sq = Rot([(cx.sbuf([128, QBW], F32), Buf()) for _ in range(2)])
        rstd = (cx.sbuf([128, QBW], F32), Buf())
        ost = Rot([(cx.sbuf([128, QBW], BF16), Buf()) for _ in range(2)])
        import os as _os
        for h in range(min(HA, int(_os.environ.get('ATT_HEADS', HA)))):
            vr = 2 * WA + h * 256
            cx.dma(cx.sp, vT[0][:], pT[vr:vr + 256, :].rearrange("(t p) s -> p t s", p=128), writes=[vT[1]])
            for i in range(2):
                kr = WA + (2 * h + i) * 128
                qr = (2 * h + i) * 128
                cx.dma(cx.sp, kq[i][0][0][:], pT[kr:kr + 128, :], writes=[kq[i][0][1]])
                cx.dma(cx.sp, kq[i][1][0][:], pT[qr:qr + 128, :], writes=[kq[i][1][1]])
            for kb in range(KB):
                tp, tpb = tps.next()
                tpv = tp[:].bitcast(BF16)
                for t in range(2):
                    cx.op(cx.pe, lambda e: e.transpose(out=tpv[:, t * 128:(t + 1) * 128],
                                                       in_=vT[0][:, t, kb * 128:(kb + 1) * 128],
                                                       identity=T["ident"][:]),
                          reads=[vT[1]], writes=[tpb], inc=(t == 1))
                cx.op(cx.dve, lambda e: e.tensor_copy(out=vtok[0][:, kb, :], in_=tpv[:, 0:256]),
                      reads=[tpb], writes=[vtok[1]])
            def gen_S(qb, i):
                qs = slice(qb * QBW, (qb + 1) * QBW)
                for kb in range(KB):
                    sp_, spb = sps.next()
                    cx.op(cx.pe, lambda e: e.matmul(sp_[:, :QBW], lhsT=kq[i][0][0][:, kb * 128:(kb + 1) * 128],
                                                    rhs=kq[i][1][0][:, qs], start=True, stop=True),
                          reads=[kq[i][0][1], kq[i][1][1]], writes=[spb])
                    cx.op(cx.act, lambda e: e.activation(out=E[i][0][:, kb, :], in_=sp_[:, :QBW],
                                                         func=AF.Exp, scale=scale),
                          reads=[spb], writes=[E[i][1]])
                    yield

            def gen_PV(qb, i):
                for t in range(3):
                    po, pob = pos[i][t]
                    for kb in range(KB):
                        lhsT = vtok[0][:, kb, t * 128:(t + 1) * 128] if t < 2 else T["ones_bf"][:]
                        cx.op(cx.pe, lambda e: e.matmul(po[:, :QBW], lhsT=lhsT, rhs=E[i][0][:, kb, :],
                                                        start=(kb == 0), stop=(kb == KB - 1)),
                              reads=[vtok[1], E[i][1]], writes=[pob], inc=(kb == KB - 1))
                        yield

            def combine(qb):
                qs = slice(qb * QBW, (qb + 1) * QBW)
                for i in range(2):
                    cx.op(cx.dve, lambda e: e.reciprocal(out=rz[i][0][:], in_=pos[i][2][0][:, :QBW]),
                          reads=[pos[i][2][1]], writes=[rz[i][1]])
                for t in range(2):
                    a, ab = ta.next()
                    b_, bb = tb_.next()
                    cx.op(cx.dve, lambda e: e.tensor_tensor(out=a[:], in0=pos[0][t][0][:, :QBW], in1=rz[0][0][:],
                                                            op=ALU.mult), reads=[pos[0][t][1], rz[0][1]], writes=[ab])
                    cx.op(cx.dve, lambda e: e.tensor_tensor(out=b_[:], in0=pos[1][t][0][:, :QBW], in1=rz[1][0][:],
                                                            op=ALU.mult), reads=[pos[1][t][1], rz[1][1]], writes=[bb])
                    cx.op(cx.dve, lambda e: e.scalar_tensor_tensor(out=od[t][0][:], in0=b_[:], scalar=neg_lam,
                                                                   in1=a[:], op0=ALU.mult, op1=ALU.add),
                          reads=[ab, bb, ls[1]], writes=[od[t][1]])
                ssp, sspb = sps.next()
                for t in range(2):
                    s_, sb_ = sq.next()
                    cx.op(cx.act, lambda e: e.activation(out=s_[:], in_=od[t][0][:], func=AF.Square),
                          reads=[od[t][1]], writes=[sb_])
                    cx.op(cx.pe, lambda e: e.matmul(ssp[:, :QBW], lhsT=T["ones_f32"][:], rhs=s_[:],
                                                    start=(t == 0), stop=(t == 1)), reads=[sb_], writes=[sspb])
                cx.op(cx.act, lambda e: e.activation(out=rstd[0][:], in_=ssp[:, :QBW], func=AF.Sqrt,
                                                     bias=T["eps"][:], scale=1.0 / 256), reads=[sspb], writes=[rstd[1]])
                cx.op(cx.dve, lambda e: e.reciprocal(out=rstd[0][:], in_=rstd[0][:]), reads=[rstd[1]], writes=[rstd[1]])
                for t in range(2):
                    o_, ob = ost.next()
                    cx.op(cx.dve, lambda e: e.scalar_tensor_tensor(out=o_[:], in0=od[t][0][:], scalar=sw[0][:, t:t + 1],
                                                                   in1=rstd[0][:], op0=ALU.mult, op1=ALU.mult),
                          reads=[od[t][1], rstd[1], sw[1]], writes=[ob])
                    r0 = h * 256 + t * 128
                    cx.dma(cx.sp, aT[r0:r0 + 128, qs], o_[:], reads=[ob])

            units = [(qb, i) for qb in range(NQB) for i in range(2)]
            for _ in gen_S(*units[0]):
                pass
            for n, (qb, i) in enumerate(units):
                pv = gen_PV(qb, i)
                nx = gen_S(*units[n + 1]) if n + 1 < len(units) else None
                pv_alive, nx_alive = True, nx is not None
                while pv_alive or nx_alive:
                    if nx_alive:
                        try:
                            next(nx)
                        except StopIteration:
                            nx_alive = False
                    for _ in range(3):
                        if pv_alive:
                            try:
                                next(pv)
                            except StopIteration:
                                pv_alive = False
                if i == 1:
                    combine(qb)


def hgrn_phase(cx, cfg, l, T, P):
    S, WA, WB, HB, DEPTH = cfg.S, cfg.WA, cfg.WB, cfg.HB, cfg.DEPTH
    TBH = min(512, S)
    NBLK = S // TBH
    NCH = TBH // 64
    NK = TBH // 128
    G2 = 2 * HB
    pT = T["pT"].ap()
    sgT = T["sgT"].ap()
    bT = T["bT"].ap()
    X = mybir.AxisListType.X
    with Scope(cx):
        lg = P["lbl"]
        ex = (cx.sbuf([128, G2, DEPTH], F32), Buf())
        ssum = (cx.sbuf([128, G2], F32), Buf())
        lb = (cx.sbuf([128, G2], F32), Buf())
        oml = (cx.sbuf([128, G2], F32), Buf())
        noml = (cx.sbuf([128, G2], F32), Buf())
        cx.op(cx.act, lambda e: e.activation(out=ex[0][:], in_=lg.rearrange("p (g l) -> p g l", l=DEPTH), func=AF.Exp),
              writes=[ex[1]])
        cx.op(cx.dve, lambda e: e.reduce_sum(out=ssum[0][:], in_=ex[0][:], axis=X), reads=[ex[1]], writes=[ssum[1]])
        cx.op(cx.dve, lambda e: e.reciprocal(out=ssum[0][:], in_=ssum[0][:]), reads=[ssum[1]], writes=[ssum[1]])
        cx.op(cx.dve, lambda e: e.memset(lb[0][:], 0.0), writes=[lb[1]])
        for j in range(1, l + 1):
            cx.op(cx.dve, lambda e: e.tensor_tensor(out=lb[0][:], in0=lb[0][:], in1=ex[0][:, :, j], op=ALU.add),
                  reads=[ex[1], lb[1]], writes=[lb[1]])
        cx.op(cx.dve, lambda e: e.tensor_tensor(out=lb[0][:], in0=lb[0][:], in1=ssum[0][:], op=ALU.mult),
              reads=[ssum[1], lb[1]], writes=[lb[1]])
        cx.op(cx.dve, lambda e: e.tensor_scalar(out=oml[0][:], in0=lb[0][:], scalar1=-1.0, scalar2=1.0,
                                                op0=ALU.mult, op1=ALU.add), reads=[lb[1]], writes=[oml[1]])
        cx.op(cx.dve, lambda e: e.tensor_scalar(out=noml[0][:], in0=lb[0][:], scalar1=-1.0, scalar2=None,
                                                op0=ALU.add), reads=[lb[1]], writes=[noml[1]])
        ones_row = (cx.sbuf([128, TBH], BF16), Buf())
        cx.op(cx.dve, lambda e: e.memset(ones_row[0][:], 1.0), writes=[ones_row[1]])

        def mk(shape, dt):
            return (cx.sbuf(shape, dt), Buf())
        HP = 2 if HB % 2 == 0 else 1
        ch = []
        for d in range(2 * HP):
            ch.append(dict(
                sig=mk([128, TBH], F32), g=mk([128, TBH], F32), G=mk([128, TBH], F32), pex=mk([128, TBH], F32),
                ek=mk([128, TBH], F32), ex=mk([128, TBH], F32), kk=mk([128, TBH], BF16), Kh=mk([128, TBH], BF16),
                Qh=mk([128, TBH], BF16), q=mk([128, TBH], BF16), ib=mk([128, TBH], BF16),
                KV=mk([128, NK, 256], BF16), AT=mk([128, NK, 128], BF16),
                tot=mk([128, NCH], F32), dcc=mk([128, NCH], F32), S=mk([128, 128], F32), Sbf=mk([128, 128], BF16)))
        o_accs = [mk([128, S], F32) for _ in range(HP)]
        tpk = Rot([(cx.psum([128, 1024], BF16), Buf()) for _ in range(1)])
        scp = Rot([(cx.psum([128, 512], F32), Buf()) for _ in range(1)])
        pops = [(cx.psum([128, 512], F32), Buf()) for _ in range(2 * HP)]
        pstp = Rot([(cx.psum([128, 512], F32), Buf()) for _ in range(6 - 2 * HP)])
        fin_sq = mk([128, TBH], F32)
        fin_r = mk([128, TBH], F32)
        fin_t = mk([128, TBH], F32)
        fin_g = mk([128, TBH], BF16)
        fin_o = Rot([mk([128, TBH], BF16) for _ in range(2)])

        def prep(c, d, hb, j):
            t0 = j * TBH
            ts = slice(t0, t0 + TBH)
            gi = d * HB + hb
            cx.dma(cx.sp, c["sig"][0][:], sgT[d * WB + hb * 128:d * WB + hb * 128 + 128, ts], writes=[c["sig"][1]])
            cx.dma(cx.sp, c["q"][0][:], pT[3 * WA + hb * 128:3 * WA + hb * 128 + 128, ts], writes=[c["q"][1]])
            r_ib = 3 * WA + 3 * WB + hb * 128
            cx.dma(cx.sp, c["ib"][0][:], pT[r_ib:r_ib + 128, ts], writes=[c["ib"][1]])
            cx.op(cx.dve, lambda e: e.tensor_scalar(out=c["g"][0][:], in0=c["sig"][0][:], scalar1=oml[0][:, gi:gi + 1],
                                                    scalar2=lb[0][:, gi:gi + 1], op0=ALU.mult, op1=ALU.add),
                  reads=[c["sig"][1], oml[1], lb[1]], writes=[c["g"][1]])
            yield
            cx.op(cx.act, lambda e: e.activation(out=c["g"][0][:], in_=c["g"][0][:], func=AF.Ln),
                  reads=[c["g"][1]], writes=[c["g"][1]])
            yield
            cx.op(cx.dve, lambda e: e.tensor_scalar(out=c["kk"][0][:], in0=c["sig"][0][:], scalar1=noml[0][:, gi:gi + 1],
                                                    scalar2=oml[0][:, gi:gi + 1], op0=ALU.mult, op1=ALU.add),
                  reads=[c["sig"][1], oml[1], noml[1]], writes=[c["kk"][1]])
            yield
            cx.op(cx.dve, lambda e: e.tensor_tensor_scan(out=c["G"][0][:], data0=ones_row[0][:], data1=c["g"][0][:],
                                                         initial=0.0, op0=ALU.mult, op1=ALU.add),
                  reads=[c["g"][1], ones_row[1]], writes=[c["G"][1]])
            yield
            cx.op(cx.dve, lambda e: e.tensor_tensor(out=c["pex"][0][:], in0=c["G"][0][:], in1=c["g"][0][:],
                                                    op=ALU.subtract), reads=[c["G"][1], c["g"][1]], writes=[c["pex"][1]])
            yield
            G3 = c["G"][0][:].rearrange("p (c t) -> p c t", t=64)
            P3 = c["pex"][0][:].rearrange("p (c t) -> p c t", t=64)
            ek3 = c["ek"][0][:].rearrange("p (c t) -> p c t", t=64)
            if d == 0:
                ref = G3[:, :, 63:64].to_broadcast([128, NCH, 64])
                cx.op(cx.dve, lambda e: e.scalar_tensor_tensor(out=ek3, in0=G3, scalar=-1.0, in1=ref,
                                                               op0=ALU.mult, op1=ALU.add),
                      reads=[c["G"][1]], writes=[c["ek"][1]])
                yield
            else:
                ref = P3[:, :, 0:1].to_broadcast([128, NCH, 64])
                cx.op(cx.dve, lambda e: e.tensor_tensor(out=ek3, in0=P3, in1=ref, op=ALU.subtract),
                      reads=[c["pex"][1]], writes=[c["ek"][1]])
                yield
            cx.op(cx.dve, lambda e: e.tensor_tensor(out=c["tot"][0][:], in0=G3[:, :, 63], in1=P3[:, :, 0],
                                                    op=ALU.subtract), reads=[c["G"][1], c["pex"][1]], writes=[c["tot"][1]])
            yield
            cx.op(cx.act, lambda e: e.activation(out=c["dcc"][0][:], in_=c["tot"][0][:], func=AF.Exp),
                  reads=[c["tot"][1]], writes=[c["dcc"][1]])
            yield
            cx.op(cx.act, lambda e: e.activation(out=c["ex"][0][:], in_=c["ek"][0][:], func=AF.Exp, scale=-1.0),
                  reads=[c["ek"][1]], writes=[c["ex"][1]])
            yield
            cx.op(cx.dve, lambda e: e.tensor_tensor(out=c["Qh"][0][:], in0=c["q"][0][:], in1=c["ex"][0][:], op=ALU.mult),
                  reads=[c["q"][1], c["ex"][1]], writes=[c["Qh"][1]])
            yield
            cx.op(cx.act, lambda e: e.activation(out=c["ex"][0][:], in_=c["ek"][0][:], func=AF.Exp),
                  reads=[c["ek"][1]], writes=[c["ex"][1]])
            yield
            cx.op(cx.dve, lambda e: e.tensor_tensor(out=c["Kh"][0][:], in0=c["kk"][0][:], in1=c["ex"][0][:], op=ALU.mult),
                  reads=[c["kk"][1], c["ex"][1]], writes=[c["Kh"][1]])
            yield
            mask = T["mask_f"] if d == 0 else T["mask_b"]
            for kb in range(NK):
                ks = slice(kb * 128, (kb + 1) * 128)
                tp, tpb = tpk.next()
                tpv = tp[:]
                cx.op(cx.pe, lambda e: e.transpose(out=tpv[:, 0:128], in_=c["Kh"][0][:, ks], identity=T["ident"][:]),
                      reads=[c["Kh"][1]], writes=[tpb], inc=False)
                cx.op(cx.pe, lambda e: e.transpose(out=tpv[:, 128:256], in_=c["ib"][0][:, ks], identity=T["ident"][:]),
                      reads=[c["ib"][1]], writes=[tpb])
                if kb % 2 == 0:
                    cx.op(cx.act, lambda e: e.activation(out=c["KV"][0][:, kb, :], in_=tpv[:, 0:256], func=AF.Copy),
                          reads=[tpb], writes=[c["KV"][1]])
                else:
                    cx.op(cx.dve, lambda e: e.tensor_copy(out=c["KV"][0][:, kb, :], in_=tpv[:, 0:256]),
                          reads=[tpb], writes=[c["KV"][1]])
                sc, scb = scp.next()
                cx.op(cx.pe, lambda e: e.matmul(sc[:, 0:128], lhsT=c["Kh"][0][:, ks], rhs=c["Qh"][0][:, ks],
                                                start=True, stop=True), reads=[c["Kh"][1], c["Qh"][1]], writes=[scb])
                cx.op(cx.dve, lambda e: e.tensor_tensor(out=c["AT"][0][:, kb, :], in0=sc[:, 0:128], in1=mask[:],
                                                        op=ALU.mult), reads=[scb], writes=[c["AT"][1]])
                yield

        def seq(c, d, j, first_flags, written, o_acc):
            kbs = range(NK) if d == 0 else range(NK - 1, -1, -1)
            for kb in kbs:
                po, pob = pops[ch.index(c)]
                cx.op(cx.pe, lambda e: e.matmul(po[:, 0:128], lhsT=c["KV"][0][:, kb, 128:256], rhs=c["AT"][0][:, kb, :],
                                                start=True, stop=False), reads=[c["KV"][1], c["AT"][1]], writes=[pob])
                halves = (0, 1) if d == 0 else (1, 0)
                for hi, hf in enumerate(halves):
                    cidx = kb * 2 + hf
                    cols = slice(hf * 64, hf * 64 + 64)
                    toks = slice(kb * 128 + hf * 64, kb * 128 + hf * 64 + 64)
                    if first_flags[d]:
                        first_flags[d] = False
                        cx.op(cx.dve, lambda e: e.memset(c["S"][0][:], 0.0), writes=[c["S"][1]])
                    else:
                        cx.op(cx.dve, lambda e: e.tensor_scalar(out=c["S"][0][:], in0=c["S"][0][:],
                                                                scalar1=c["dcc"][0][:, cidx:cidx + 1], scalar2=None,
                                                                op0=ALU.mult),
                              reads=[c["S"][1], c["dcc"][1]], writes=[c["S"][1]])
                    yield
                    cx.op(cx.act, lambda e: e.activation(out=c["Sbf"][0][:], in_=c["S"][0][:], func=AF.Copy),
                          reads=[c["S"][1]], writes=[c["Sbf"][1]])
                    yield
                    cx.op(cx.pe, lambda e: e.matmul(po[:, cols], lhsT=c["Sbf"][0][:], rhs=c["Qh"][0][:, toks],
                                                    start=False, stop=(hi == 1)),
                          reads=[c["Sbf"][1], c["Qh"][1]], writes=[pob])
                    pst, pstb = pstp.next()
                    cx.op(cx.pe, lambda e: e.matmul(pst[:, 0:128], lhsT=c["KV"][0][hf * 64:hf * 64 + 64, kb, 0:128],
                                                    rhs=c["KV"][0][hf * 64:hf * 64 + 64, kb, 128:256], start=True, stop=True),
                          reads=[c["KV"][1]], writes=[pstb])
                    cx.op(cx.dve, lambda e: e.tensor_tensor(out=c["S"][0][:], in0=c["S"][0][:], in1=pst[:, 0:128],
                                                            op=ALU.add), reads=[c["S"][1], pstb], writes=[c["S"][1]])
                    yield
                og = slice(j * TBH + kb * 128, j * TBH + kb * 128 + 128)
                key = (j, kb)
                if key not in written:
                    written.add(key)
                    cx.op(cx.act, lambda e: e.activation(out=o_acc[0][:, og], in_=po[:, 0:128], func=AF.Copy),
                          reads=[pob], writes=[o_acc[1]])
                else:
                    cx.op(cx.dve, lambda e: e.tensor_tensor(out=o_acc[0][:, og], in0=o_acc[0][:, og], in1=po[:, 0:128],
                                                            op=ALU.add), reads=[pob, o_acc[1]], writes=[o_acc[1]])
                yield

        for hb0 in range(0, HB, HP):
            flags = [[True, True] for _ in range(HP)]
            writtens = [set() for _ in range(HP)]
            for step in range(NBLK):
                js = (step, NBLK - 1 - step)
                pgens = [prep(ch[hp * 2 + d], d, hb0 + hp, js[d]) for hp in range(HP) for d in range(2)]
                palive = [True] * len(pgens)
                while any(palive):
                    for gi in range(len(pgens)):
                        if palive[gi]:
                            try:
                                next(pgens[gi])
                            except StopIteration:
                                palive[gi] = False
                gens = [seq(ch[hp * 2 + d], d, js[d], flags[hp], writtens[hp], o_accs[hp])
                        for hp in range(HP) for d in range(2)]
                alive = [True] * len(gens)
                while any(alive):
                    for gi in range(len(gens)):
                        if alive[gi]:
                            try:
                                next(gens[gi])
                            except StopIteration:
                                alive[gi] = False
            for hp in range(HP):
              hb = hb0 + hp
              o_acc = o_accs[hp]
              r_g = 3 * WA + 4 * WB + hb * 128
              for j in range(NBLK):
                  ts = slice(j * TBH, (j + 1) * TBH)
                  cx.dma(cx.sp, fin_g[0][:], pT[r_g:r_g + 128, ts], writes=[fin_g[1]])
                  cx.op(cx.act, lambda e: e.activation(out=fin_sq[0][:], in_=o_acc[0][:, ts], func=AF.Square),
                        reads=[o_acc[1]], writes=[fin_sq[1]])
                  ps_, psb = scp.next()
                  cx.op(cx.pe, lambda e: e.matmul(ps_[:, :TBH], lhsT=T["ones_f32"][:], rhs=fin_sq[0][:], start=True, stop=True),
                        reads=[fin_sq[1]], writes=[psb])
                  cx.op(cx.act, lambda e: e.activation(out=fin_r[0][:], in_=ps_[:, :TBH], func=AF.Sqrt, bias=T["eps"][:],
                                                       scale=1.0 / 128), reads=[psb], writes=[fin_r[1]])
                  cx.op(cx.dve, lambda e: e.reciprocal(out=fin_r[0][:], in_=fin_r[0][:]), reads=[fin_r[1]], writes=[fin_r[1]])
                  cx.op(cx.dve, lambda e: e.scalar_tensor_tensor(out=fin_t[0][:], in0=o_acc[0][:, ts],
                                                                 scalar=P["hnw"][:, l:l + 1], in1=fin_r[0][:],
                                                                 op0=ALU.mult, op1=ALU.mult),
                        reads=[o_acc[1], fin_r[1]], writes=[fin_t[1]])
                  fo, fob = fin_o.next()
                  cx.op(cx.dve, lambda e: e.tensor_tensor(out=fo[:], in0=fin_t[0][:], in1=fin_g[0][:], op=ALU.mult),
                        reads=[fin_t[1], fin_g[1]], writes=[fob])
                  cx.dma(cx.sp, bT[hb * 128:hb * 128 + 128, ts], fo[:], reads=[fob])


def branch_phase(cx, cfg, l, T, which):
    D, S, WA, WB = cfg.D, cfg.S, cfg.WA, cfg.WB
    pT = T["pT"].ap()
    mA = T["mA"].ap()
    mT = T["mT"].ap()
    g0 = 3 * WA + 5 * WB + (0 if which == 0 else D)
    with Scope(cx):
        gt = Rot([(cx.sbuf([128, 512], BF16), Buf()) for _ in range(6)])
        at = Rot([(cx.sbuf([128, 512], F32), Buf()) for _ in range(6)])
        of = Rot([(cx.sbuf([128, 512], F32), Buf()) for _ in range(6)])
        ob = Rot([(cx.sbuf([128, 512], BF16), Buf()) for _ in range(6)])

        def epi(pt, pb, n0, t0, w):
            g, gb = gt.next()
            cx.dma(cx.act, g[:, :w], pT[g0 + n0:g0 + n0 + 128, t0:t0 + w], writes=[gb])
            if which == 0:
                o, obuf = of.next()
                cx.op(cx.dve, lambda e: e.tensor_tensor(out=o[:, :w], in0=pt[:, :w], in1=g[:, :w], op=ALU.mult),
                      reads=[pb, gb], writes=[obuf])
                cx.dma(cx.sp, mA[n0:n0 + 128, t0:t0 + w], o[:, :w], reads=[obuf])
            else:
                a, ab = at.next()
                cx.dma(cx.act, a[:, :w], mA[n0:n0 + 128, t0:t0 + w], writes=[ab])
                o, obuf = of.next()
                cx.op(cx.dve, lambda e: e.tensor_tensor(out=o[:, :w], in0=pt[:, :w], in1=g[:, :w], op=ALU.mult),
                      reads=[pb, gb], writes=[obuf])
                o2, o2b = ob.next()
                cx.op(cx.dve, lambda e: e.tensor_tensor(out=o2[:, :w], in0=o[:, :w], in1=a[:, :w], op=ALU.add),
                      reads=[obuf, ab], writes=[o2b])
                cx.dma(cx.sp, mT[n0:n0 + 128, t0:t0 + w], o2[:, :w], reads=[o2b])

        if which == 0:
            gemm_phase(cx, cfg, T["w_branch_a"].ap()[l], T["aT"].ap(), Buf(), WA, D, S, epi)
        else:
            gemm_phase(cx, cfg, T["w_branch_b"].ap()[l], T["bT"].ap(), Buf(), WB, D, S, epi)


def resid_gemm_phase(cx, cfg, W, xin, K, T):
    D, S = cfg.D, cfg.S
    xT = T["xT"].ap()
    with Scope(cx):
        depth = 6 if K <= 4096 else 3
        xt = Rot([(cx.sbuf([128, 512], F32), Buf()) for _ in range(depth)])
        ot = Rot([(cx.sbuf([128, 512], F32), Buf()) for _ in range(depth)])

        def epi(pt, pb, n0, t0, w):
            x_, xb = xt.next()
            cx.dma(cx.act, x_[:, :w], xT[n0:n0 + 128, t0:t0 + w], writes=[xb])
            o, obuf = ot.next()
            cx.op(cx.dve, lambda e: e.tensor_tensor(out=o[:, :w], in0=pt[:, :w], in1=x_[:, :w], op=ALU.add),
                  reads=[pb, xb], writes=[obuf])
            cx.dma(cx.sp, xT[n0:n0 + 128, t0:t0 + w], o[:, :w], reads=[obuf])

        gemm_phase(cx, cfg, W, xin, Buf(), K, D, S, epi)


def up_phase(cx, cfg, l, T):
    D, S, DFF = cfg.D, cfg.S, cfg.DFF
    uT = T["uT"].ap()
    with Scope(cx):
        st = Rot([(cx.sbuf([128, 512], BF16), Buf()) for _ in range(4)])
        evac = make_evac(cx)

        def epi(pt, pb, n0, t0, w):
            s_, sb_ = st.next()
            evac(s_[:, :w], pt[:, :w], [pb], [sb_])
            cx.dma(cx.sp, uT[n0:n0 + 128, t0:t0 + w], s_[:, :w], reads=[sb_])

        gemm_phase(cx, cfg, T["w_up"].ap()[l], T["hT"].ap(), Buf(), D, 2 * DFF, S, epi)


def conv_phase(cx, cfg, l, T, P):
    S, DFF = cfg.S, cfg.DFF
    NFC = 2 * DFF // 128
    NG = DFF // 128
    TW = min(512, S)
    uT = T["uT"].ap()
    actT = T["actT"].ap()
    cw, cb = P["convw"], P["convb"]
    with Scope(cx):
        ug = [(cx.sbuf([128, S + 2], BF16), Buf()) for _ in range(2)]
        uv = [(cx.sbuf([128, S + 2], BF16), Buf()) for _ in range(2)]
        dg = [(cx.sbuf([128, 6, 128], BF16), Buf()) for _ in range(2)]
        sg = Rot([(cx.sbuf([128, TW], F32), Buf()) for _ in range(3)])
        oo = [(cx.sbuf([128, S], BF16), Buf()) for _ in range(2)]
        pg = Rot([(cx.psum([128, 512], F32), Buf()) for _ in range(3)])
        pv = Rot([(cx.psum([128, 512], F32), Buf()) for _ in range(3)])
        for t_, b_ in ug + uv:
            cx.op(cx.dve, lambda e: e.memset(t_[:, 0:1], 0.0), writes=[b_])
            cx.op(cx.dve, lambda e: e.memset(t_[:, S + 1:S + 2], 0.0), writes=[b_])
        for c in range(NG):
            k = c % 2
            cx.dma(cx.sp, ug[k][0][:, 1:S + 1], uT[c * 128:(c + 1) * 128, :], writes=[ug[k][1]])
            cx.dma(cx.sp, uv[k][0][:, 1:S + 1], uT[DFF + c * 128:DFF + (c + 1) * 128, :], writes=[uv[k][1]])
            for j in range(3):
                for gi, cc in enumerate((c, NG + c)):
                    col = (l * 3 + j) * NFC + cc
                    cx.op(cx.dve, lambda e: e.tensor_scalar(out=dg[k][0][:, gi * 3 + j, :], in0=T["ident"],
                                                            scalar1=cw[:, col:col + 1], scalar2=None, op0=ALU.mult),
                          writes=[dg[k][1]])
            for tb in range(S // TW):
                a, ab = pg.next()
                v, vb = pv.next()
                for j in range(3):
                    cx.op(cx.pe, lambda e: e.matmul(a[:, :TW], lhsT=dg[k][0][:, j, :],
                                                    rhs=ug[k][0][:, tb * TW + j:tb * TW + j + TW],
                                                    start=(j == 0), stop=(j == 2)),
                          reads=[dg[k][1], ug[k][1]], writes=[ab], inc=(j == 2))
                for j in range(3):
                    cx.op(cx.pe, lambda e: e.matmul(v[:, :TW], lhsT=dg[k][0][:, 3 + j, :],
                                                    rhs=uv[k][0][:, tb * TW + j:tb * TW + j + TW],
                                                    start=(j == 0), stop=(j == 2)),
                          reads=[dg[k][1], uv[k][1]], writes=[vb], inc=(j == 2))
                s_, sb_ = sg.next()
                cx.op(cx.act, lambda e: e.activation(out=s_[:], in_=a[:, :TW], func=AF.Silu,
                                                     bias=cb[:, l * NFC + c:l * NFC + c + 1]),
                      reads=[ab], writes=[sb_])
                cx.op(cx.dve, lambda e: e.scalar_tensor_tensor(out=oo[k][0][:, tb * TW:(tb + 1) * TW], in0=v[:, :TW],
                                                               scalar=cb[:, l * NFC + NG + c:l * NFC + NG + c + 1],
                                                               in1=s_[:], op0=ALU.add, op1=ALU.mult),
                      reads=[vb, sb_], writes=[oo[k][1]])
            cx.dma(cx.sp, actT[c * 128:(c + 1) * 128, :], oo[k][0][:], reads=[oo[k][1]])


def param_layout(cfg):
    DC = cfg.D // 128
    NFC = 2 * cfg.DFF // 128
    segs = [("n1", cfg.DEPTH * DC), ("n2", cfg.DEPTH * DC), ("nf", DC), ("lam", cfg.DEPTH * 512),
            ("subln", cfg.DEPTH * 2), ("lbl", 2 * cfg.HB * cfg.DEPTH), ("hnw", cfg.DEPTH),
            ("convw", cfg.DEPTH * 3 * NFC), ("convb", cfg.DEPTH * NFC), ("mask_f", 128), ("mask_b", 128)]
    off, o = {}, 0
    for n, w in segs:
        off[n] = (o, o + w)
        o += w
    return off, o


def build(cfg, nlayers=None, stop_after=None):
    D, S, DEPTH = cfg.D, cfg.S, cfg.DEPTH
    nlayers = DEPTH if nlayers is None else nlayers
    nc = bass.Bass("TRN2", target_bir_lowering=False)
    T = {}

    def din(name, shape, dt=F32):
        T[name] = nc.dram_tensor(name, list(shape), dt, kind="ExternalInput")

    din("x_in", [D, S])
    din("w_in", [DEPTH, D, cfg.N_IN])
    din("w_branch_a", [DEPTH, cfg.WA, D])
    din("w_branch_b", [DEPTH, cfg.WB, D])
    din("w_out", [DEPTH, D, D])
    din("w_up", [DEPTH, D, 2 * cfg.DFF])
    din("w_down", [DEPTH, cfg.DFF, D])
    poff, NP = param_layout(cfg)
    din("params", [128, NP])
    din("c_cos", [128, S])
    din("c_sin", [128, S])
    din("c_bf", [128, 256], BF16)
    T["out"] = nc.dram_tensor("out", [D, S], F32, kind="ExternalOutput")
    for name, shape, dt in (("xT", [D, S], F32), ("hT", [D, S], BF16), ("pT", [cfg.N_IN, S], BF16),
                            ("sgT", [2 * cfg.WB, S], F32), ("aT", [cfg.WA, S], BF16), ("bT", [cfg.WB, S], BF16),
                            ("mA", [D, S], F32), ("mT", [D, S], BF16), ("uT", [2 * cfg.DFF, S], BF16),
                            ("actT", [cfg.DFF, S], BF16)):
        T[name] = nc.dram_tensor(name, shape, dt)
    cx = Ctx(nc)
    with cx.stack:
        par = cx.sbuf([128, NP], F32, "par")
        cbf = cx.sbuf([128, 256], BF16, "cbf")
        T["ones_f32"] = cx.sbuf([128, 128], F32, "ones_f32")
        T["ones_bf"] = cx.sbuf([128, 128], BF16, "ones_bf")
        T["eps"] = cx.sbuf([128, 1], F32, "eps")
        cb = Buf()
        cx.dma(cx.sp, par[:], T["params"].ap(), writes=[cb])
        cx.dma(cx.sp, cbf[:], T["c_bf"].ap(), writes=[cb])
        cx.op(cx.dve, lambda e: e.memset(T["ones_f32"][:], 1.0), writes=[Buf()])
        cx.op(cx.dve, lambda e: e.memset(T["ones_bf"][:], 1.0), writes=[Buf()])
        cx.op(cx.dve, lambda e: e.memset(T["eps"][:], cfg.EPS), writes=[Buf()])
        P = {n: par[:, a:b] for n, (a, b) in poff.items()}
        T["ident"] = cbf[:, 0:128]
        T["rot"] = cbf[:, 128:256]
        T["mask_f"] = P["mask_f"]
        T["mask_b"] = P["mask_b"]
        xi = T["x_in"].ap()
        xo = T["xT"].ap()
        nparts = 8
        rows = D // nparts
        for i in range(nparts):
            cx.dma(cx.sp, xo[i * rows:(i + 1) * rows, :], xi[i * rows:(i + 1) * rows, :])
        cx.barrier()
        DC = D // 128
        done = False
        for l in range(nlayers):
            rmsnorm_phase(cx, cfg, T["xT"].ap(), Buf(), P["n1"][:, l * DC:(l + 1) * DC], T["hT"].ap(), Buf(), T)
            proj_phase(cx, cfg, l, T)
            if stop_after == "proj":
                break
            attn_phase(cx, cfg, l, T, P)
            if stop_after == "attn":
                break
            hgrn_phase(cx, cfg, l, T, P)
            if stop_after == "hgrn":
                break
            branch_phase(cx, cfg, l, T, 0)
            branch_phase(cx, cfg, l, T, 1)
            resid_gemm_phase(cx, cfg, T["w_out"].ap()[l], T["mT"].ap(), D, T)
            if stop_after == "mixer":
                break
            rmsnorm_phase(cx, cfg, T["xT"].ap(), Buf(), P["n2"][:, l * DC:(l + 1) * DC], T["hT"].ap(), Buf(), T)
            up_phase(cx, cfg, l, T)
            conv_phase(cx, cfg, l, T, P)
            resid_gemm_phase(cx, cfg, T["w_down"].ap()[l], T["actT"].ap(), cfg.DFF, T)
        rmsnorm_phase(cx, cfg, T["xT"].ap(), Buf(), P["nf"], T["out"].ap(), Buf(), T, out_f32=True)
        cx.finish([])
    return nc, T


def host_consts(cfg):
    import ml_dtypes
    S = cfg.S
    inv = 1.0 / (10000.0 ** (np.arange(0, 128, 2, dtype=np.float32) / 128))
    ang = np.arange(S, dtype=np.float32)[:, None] * inv[None, :]
    cos = np.cos(ang).astype(np.float32).T
    sin = np.sin(ang).astype(np.float32).T
    c_cos = np.ascontiguousarray(np.concatenate([cos, cos], 0))
    c_sin = np.ascontiguousarray(np.concatenate([sin, sin], 0))
    ident = np.eye(128, dtype=np.float32)
    rot = np.zeros((128, 128), np.float32)
    for m in range(64):
        rot[m + 64, m] = -1.0
        rot[m, m + 64] = 1.0
    c_bf = np.concatenate([ident, rot], 1).astype(ml_dtypes.bfloat16)
    s_ = np.arange(128)[:, None]
    t_ = np.arange(128)[None, :]
    same = (s_ // 64) == (t_ // 64)
    mask_f = (same & (s_ <= t_)).astype(np.float32)
    mask_b = (same & (s_ >= t_)).astype(np.float32)
    return c_cos, c_sin, c_bf, mask_f, mask_b


def host_params(cfg, inp):
    D, DEPTH, HB = cfg.D, cfg.DEPTH, cfg.HB
    DC = D // 128
    NFC = 2 * cfg.DFF // 128
    poff, NP = param_layout(cfg)
    par = np.zeros((128, NP), np.float32)

    def put(name, arr):
        a, b = poff[name]
        par[:, a:b] = arr.reshape(128, b - a)

    f = lambda k: np.asarray(inp[k], np.float32)
    put("n1", f("norm1_w").reshape(DEPTH, DC, 128).transpose(2, 0, 1))
    put("n2", f("norm2_w").reshape(DEPTH, DC, 128).transpose(2, 0, 1))
    put("nf", f("final_norm_w").reshape(DC, 128).transpose(1, 0))
    put("lam", np.broadcast_to(f("diff_lambda").reshape(1, DEPTH * 512), (128, DEPTH * 512)))
    put("subln", f("diff_subln_w").reshape(DEPTH, 2, 128).transpose(2, 0, 1))
    put("lbl", f("hgrn_lb_logits").reshape(DEPTH, 2, HB, 128).transpose(3, 1, 2, 0))
    put("hnw", f("hgrn_norm_w").reshape(DEPTH, 128).transpose(1, 0))
    put("convw", f("conv_w").reshape(DEPTH, 3, NFC, 128).transpose(3, 0, 1, 2))
    put("convb", f("conv_b").reshape(DEPTH, NFC, 128).transpose(2, 0, 1))
    _, _, _, mask_f, mask_b = host_consts(cfg)
    put("mask_f", mask_f)
    put("mask_b", mask_b)
    return par


_CACHE = {}


def kernel(**inputs):
    x = np.asarray(inputs["x"], np.float32)
    B, S, D = x.shape
    DEPTH = np.asarray(inputs["w_in"]).shape[0]
    cfg = Cfg(D=D, S=S, DEPTH=DEPTH)
    key = (D, S, DEPTH)
    if key not in _CACHE:
        _CACHE[key] = build(cfg)
    nc, T = _CACHE[key]
    c_cos, c_sin, c_bf, _, _ = host_consts(cfg)
    par = host_params(cfg, inputs)
    ncores = B
    shared = {k: np.ascontiguousarray(np.asarray(inputs[k], np.float32)) for k in
              ("w_in", "w_branch_a", "w_branch_b", "w_out", "w_up", "w_down")}
    in_maps = []
    for c in range(ncores):
        b = c % B
        m = dict(shared)
        m["x_in"] = np.ascontiguousarray(x[b].T)
        m["params"] = par
        m["c_cos"] = c_cos
        m["c_sin"] = c_sin
        m["c_bf"] = c_bf
        in_maps.append(m)
    res = run_bass_kernel_spmd(nc, in_maps, core_ids=list(range(ncores)))
    out = np.stack([np.ascontiguousarray(res.results[b]["out"].T) for b in range(B)], 0)
    return out.astype(np.float32)
```
